# Optimizing a Trainium2 kernel written in Bass

```python
import math
import jax, jax.numpy as jnp
from jax import lax
import numpy as np

D_MODEL = 2048
BATCH = 4
SEQ = 2048
DEPTH = 4
DEC_BATCH = 8
DEC_SEQ = 4
PAST_LEN = 16384
PAGE_SIZE = 128

N_A_LAYERS = DEPTH // 2
N_B_LAYERS = DEPTH - N_A_LAYERS
GROUP_CHANNELS = 16
N_GROUPS = D_MODEL // GROUP_CHANNELS
STATE_DIM = 64
HEAD_DIM = 128
V_DIM = 2 * HEAD_DIM
N_HEADS = D_MODEL // V_DIM
D_FF = 4 * D_MODEL
Q_BLOCK = 128
ATTN_SCALE = HEAD_DIM ** -0.5
NORM_EPS = 1e-6
DT_MIN = 0.001
DT_MAX = 0.1

kernel_name = 'yoco_s5_diffattn_decoder_step'


def rmsnorm(x, w):
    xf = x.astype(jnp.float32)
    xf = xf * lax.rsqrt(jnp.mean(xf * xf, axis=-1, keepdims=True) + NORM_EPS)
    return (xf * w.astype(jnp.float32)).astype(x.dtype)


def modulate(h, shift, scale):
    return h * (1 + scale[:, None, :]) + shift[:, None, :]


def cmul(ar, ai, br, bi):
    return ar * br - ai * bi, ar * bi + ai * br


def sq_relu_mlp(h, w_up, w_down):
    return jnp.square(jax.nn.relu(h @ w_up)) @ w_down


def s5_mixer(u, s0_re, s0_im, lam_re, lam_im, log_step, b_re, b_im, c_re, c_im, d_skip, glu_w):
    bsz, t = u.shape[:2]
    f32 = jnp.float32
    lr = lam_re.astype(f32)
    li = lam_im.astype(f32)
    dt = jnp.exp(log_step.astype(f32))[:, None]
    mag = jnp.exp(lr * dt)
    abar_re = mag * jnp.cos(li * dt)
    abar_im = mag * jnp.sin(li * dt)
    den = lr * lr + li * li
    xr = abar_re - 1.0
    f_re = (xr * lr + abar_im * li) / den
    f_im = (abar_im * lr - xr * li) / den
    bb_re, bb_im = cmul(f_re[..., None], f_im[..., None], b_re.astype(f32), b_im.astype(f32))
    ug = u.astype(f32).reshape(bsz, t, N_GROUPS, GROUP_CHANNELS)
    bu_re = jnp.einsum('btgc,gnc->tbgn', ug, bb_re)
    bu_im = jnp.einsum('btgc,gnc->tbgn', ug, bb_im)
    i_re, i_im = cmul(abar_re, abar_im, s0_re.astype(f32), s0_im.astype(f32))
    bu_re = bu_re.at[0].add(i_re)
    bu_im = bu_im.at[0].add(i_im)
    a_re = jnp.broadcast_to(abar_re, (t, 1, N_GROUPS, STATE_DIM))
    a_im = jnp.broadcast_to(abar_im, (t, 1, N_GROUPS, STATE_DIM))

    def combine(e1, e2):
        a1r, a1i, b1r, b1i = e1
        a2r, a2i, b2r, b2i = e2
        ar, ai = cmul(a2r, a2i, a1r, a1i)
        br, bi = cmul(a2r, a2i, b1r, b1i)
        return ar, ai, br + b2r, bi + b2i

    _, _, s_re, s_im = lax.associative_scan(combine, (a_re, a_im, bu_re, bu_im), axis=0)
    y = jnp.einsum('tbgn,gcn->btgc', s_re, c_re.astype(f32)) - jnp.einsum('tbgn,gcn->btgc', s_im, c_im.astype(f32))
    y = y.reshape(bsz, t, D_MODEL) + d_skip.astype(f32) * u.astype(f32)
    h = jax.nn.gelu(y).astype(u.dtype)
    val, gate = jnp.split(h @ glu_w, 2, axis=-1)
    return val * jax.nn.sigmoid(gate), s_re[-1], s_im[-1]


def shared_kv(x, c, kv_ada_w, kv_ada_b, kv_norm_w, kv_w, k_norm_w):
    bsz, t = x.shape[:2]
    shift, scale = jnp.split(c @ kv_ada_w + kv_ada_b, 2, axis=-1)
    h = modulate(rmsnorm(x, kv_norm_w), shift, scale)
    kv = h @ kv_w
    k, v = jnp.split(kv, [N_HEADS * 2 * HEAD_DIM], axis=-1)
    k = rmsnorm(k.reshape(bsz, t, N_HEADS, 2, HEAD_DIM), k_norm_w)
    return k.reshape(bsz, t, N_HEADS, 2 * HEAD_DIM), v.reshape(bsz, t, N_HEADS, V_DIM)


def diff_attend_core(q, ks, vs, masks, lam):
    scores = []
    for k, m in zip(ks, masks):
        s = jnp.einsum('bqhmd,bkhmd->bmhqk', q, k, preferred_element_type=jnp.float32) * ATTN_SCALE
        if m is not None:
            s = jnp.where(m, s, -jnp.inf)
        scores.append(s)
    p = jax.nn.softmax(jnp.concatenate(scores, axis=-1), axis=-1)
    a = p[:, 0] - lam * p[:, 1]
    out = 0
    start = 0
    for v in vs:
        n = v.shape[1]
        out = out + jnp.einsum('bhqk,bkhv->bqhv', a[..., start:start + n].astype(v.dtype), v,
                               preferred_element_type=jnp.float32)
        start += n
    return out.astype(q.dtype)


def prompt_attend(q, k, v, lam):
    bsz, t = q.shape[:2]
    nb = t // Q_BLOCK
    k5 = k.reshape(bsz, t, N_HEADS, 2, HEAD_DIM)
    qb = jnp.moveaxis(q.reshape(bsz, nb, Q_BLOCK, N_HEADS, 2, HEAD_DIM), 1, 0)
    kpos = jnp.arange(t)

    def one_block(args):
        q_blk, blk = args
        qpos = blk * Q_BLOCK + jnp.arange(Q_BLOCK)
        mask = kpos[None, :] <= qpos[:, None]
        return diff_attend_core(q_blk, [k5], [v], [mask], lam)

    out = lax.map(one_block, (qb, jnp.arange(nb)))
    return jnp.moveaxis(out, 0, 1).reshape(bsz, t, N_HEADS, V_DIM)


def make_sample_attend(k_past, v_past):
    def attend(q, k, v, lam):
        bsz, n = q.shape[:2]
        causal = jnp.arange(n)[None, :] <= jnp.arange(n)[:, None]
        k5 = k.reshape(bsz, n, N_HEADS, 2, HEAD_DIM)
        return diff_attend_core(q, [k_past, k5], [v_past, v], [None, causal], lam)
    return attend


def diff_attn_mixer(h, k, v, attend, wq, q_norm_w, lq1, lk1, lq2, lk2, subln_w, wo, lam_init):
    bsz, t = h.shape[:2]
    f32 = jnp.float32
    q = rmsnorm((h @ wq).reshape(bsz, t, N_HEADS, 2, HEAD_DIM), q_norm_w)
    lam = (jnp.exp(jnp.sum(lq1.astype(f32) * lk1.astype(f32)))
           - jnp.exp(jnp.sum(lq2.astype(f32) * lk2.astype(f32))) + lam_init)
    o = attend(q, k, v, lam)
    o = rmsnorm(o, subln_w) * (1.0 - lam_init)
    return o.reshape(bsz, t, D_MODEL) @ wo


def run_group(x, c, s_re0, s_im0, attend, p):
    new_re, new_im = [], []
    k = v = None
    for l in range(DEPTH):
        mod = c @ p['ada_w'][l] + p['ada_b'][l]
        sh1, sc1, g1, sh2, sc2, g2 = jnp.split(mod, 6, axis=-1)
        h = modulate(rmsnorm(x, p['norm_w'][l, 0]), sh1, sc1)
        if l < N_A_LAYERS:
            y, sr, si = s5_mixer(h, s_re0[l], s_im0[l], p['ssm_lambda_re'][l], p['ssm_lambda_im'][l],
                                 p['ssm_log_step'][l], p['ssm_b_re'][l], p['ssm_b_im'][l],
                                 p['ssm_c_re'][l], p['ssm_c_im'][l], p['ssm_d'][l], p['glu_w'][l])
            new_re.append(sr)
            new_im.append(si)
        else:
            j = l - N_A_LAYERS
            lam_init = 0.8 - 0.6 * math.exp(-0.3 * l)
            y = diff_attn_mixer(h, k, v, attend, p['attn_wq'][j], p['q_norm_w'][j], p['lambda_q1'][j],
                                p['lambda_k1'][j], p['lambda_q2'][j], p['lambda_k2'][j], p['subln_w'][j],
                                p['attn_wo'][j], lam_init)
        x = x + g1[:, None, :] * y
        h = modulate(rmsnorm(x, p['norm_w'][l, 1]), sh2, sc2)
        x = x + g2[:, None, :] * sq_relu_mlp(h, p['mlp_up'][l], p['mlp_down'][l])
        if l == N_A_LAYERS - 1:
            k, v = shared_kv(x, c, p['kv_ada_w'], p['kv_ada_b'], p['kv_norm_w'], p['kv_w'], p['k_norm_w'])
    return x, jnp.stack(new_re), jnp.stack(new_im), k, v


def setup_inputs(seed: int = 0) -> dict:
    key = jax.random.key(seed)
    ks = jax.random.split(key, 40)
    f32 = jnp.float32
    n_pages = PAST_LEN // PAGE_SIZE
    n_used = DEC_BATCH * n_pages
    n_pool = (5 * n_used) // 4
    inv = D_MODEL ** -0.5

    def nrm(k, shape, s):
        return jax.random.normal(k, shape, f32) * s

    page_table = jax.random.permutation(ks[6], n_pool)[:n_used].reshape(DEC_BATCH, n_pages).astype(jnp.int32)
    log_step = (math.log(DT_MIN) + jax.random.uniform(ks[12], (N_A_LAYERS, N_GROUPS), f32)
                * (math.log(DT_MAX) - math.log(DT_MIN)))
    lam_im = math.pi * jnp.arange(STATE_DIM, dtype=f32) + nrm(ks[11], (N_A_LAYERS, N_GROUPS, STATE_DIM), 0.01)
    return {
        'x_prompt': nrm(ks[0], (BATCH, SEQ, D_MODEL), 1.0),
        'x_sample': nrm(ks[1], (DEC_BATCH, DEC_SEQ, D_MODEL), 1.0),
        'state_ssm_re': nrm(ks[2], (N_A_LAYERS, DEC_BATCH, N_GROUPS, STATE_DIM), 0.3),
        'state_ssm_im': nrm(ks[3], (N_A_LAYERS, DEC_BATCH, N_GROUPS, STATE_DIM), 0.3),
        'cache_k': nrm(ks[4], (n_pool, PAGE_SIZE, N_HEADS, 2 * HEAD_DIM), 1.0),
        'cache_v': nrm(ks[5], (n_pool, PAGE_SIZE, N_HEADS, V_DIM), 1.0),
        'page_table': page_table,
        'c_prompt': nrm(ks[7], (BATCH, D_MODEL), 1.0),
        'c_sample': nrm(ks[8], (DEC_BATCH, D_MODEL), 1.0),
        'ada_w': nrm(ks[9], (DEPTH, D_MODEL, 6 * D_MODEL), 0.5 * inv),
        'ada_b': nrm(ks[10], (DEPTH, 6 * D_MODEL), 0.01),
        'norm_w': 1.0 + nrm(ks[13], (DEPTH, 2, D_MODEL), 0.01),
        'mlp_up': nrm(ks[14], (DEPTH, D_MODEL, D_FF), inv),
        'mlp_down': nrm(ks[15], (DEPTH, D_FF, D_MODEL), D_FF ** -0.5),
        'ssm_lambda_re': -0.5 + nrm(ks[16], (N_A_LAYERS, N_GROUPS, STATE_DIM), 0.01),
        'ssm_lambda_im': lam_im,
        'ssm_log_step': log_step,
        'ssm_b_re': nrm(ks[17], (N_A_LAYERS, N_GROUPS, STATE_DIM, GROUP_CHANNELS), (2 * GROUP_CHANNELS) ** -0.5),
        'ssm_b_im': nrm(ks[18], (N_A_LAYERS, N_GROUPS, STATE_DIM, GROUP_CHANNELS), (2 * GROUP_CHANNELS) ** -0.5),
        'ssm_c_re': nrm(ks[19], (N_A_LAYERS, N_GROUPS, GROUP_CHANNELS, STATE_DIM), STATE_DIM ** -0.5),
        'ssm_c_im': nrm(ks[20], (N_A_LAYERS, N_GROUPS, GROUP_CHANNELS, STATE_DIM), STATE_DIM ** -0.5),
        'ssm_d': nrm(ks[21], (N_A_LAYERS, D_MODEL), 1.0),
        'glu_w': nrm(ks[22], (N_A_LAYERS, D_MODEL, 2 * D_MODEL), inv),
        'kv_ada_w': nrm(ks[23], (D_MODEL, 2 * D_MODEL), 0.5 * inv),
        'kv_ada_b': nrm(ks[24], (2 * D_MODEL,), 0.01),
        'kv_norm_w': 1.0 + nrm(ks[25], (D_MODEL,), 0.01),
        'kv_w': nrm(ks[26], (D_MODEL, N_HEADS * (2 * HEAD_DIM + V_DIM)), inv),
        'k_norm_w': 1.0 + nrm(ks[27], (HEAD_DIM,), 0.01),
        'attn_wq': nrm(ks[28], (N_B_LAYERS, D_MODEL, N_HEADS * 2 * HEAD_DIM), inv),
        'q_norm_w': 1.0 + nrm(ks[29], (N_B_LAYERS, HEAD_DIM), 0.01),
        'lambda_q1': nrm(ks[30], (N_B_LAYERS, HEAD_DIM), 0.1),
        'lambda_k1': nrm(ks[31], (N_B_LAYERS, HEAD_DIM), 0.1),
        'lambda_q2': nrm(ks[32], (N_B_LAYERS, HEAD_DIM), 0.1),
        'lambda_k2': nrm(ks[33], (N_B_LAYERS, HEAD_DIM), 0.1),
        'subln_w': 1.0 + nrm(ks[34], (N_B_LAYERS, V_DIM), 0.01),
        'attn_wo': nrm(ks[35], (N_B_LAYERS, D_MODEL, D_MODEL), inv),
    }


def reference(x_prompt, x_sample, state_ssm_re, state_ssm_im, cache_k, cache_v, page_table, c_prompt, c_sample,
              ada_w, ada_b, norm_w, mlp_up, mlp_down, ssm_lambda_re, ssm_lambda_im, ssm_log_step,
              ssm_b_re, ssm_b_im, ssm_c_re, ssm_c_im, ssm_d, glu_w, kv_ada_w, kv_ada_b, kv_norm_w, kv_w,
              k_norm_w, attn_wq, q_norm_w, lambda_q1, lambda_k1, lambda_q2, lambda_k2, subln_w, attn_wo):
    p = {
        'ada_w': ada_w, 'ada_b': ada_b, 'norm_w': norm_w, 'mlp_up': mlp_up, 'mlp_down': mlp_down,
        'ssm_lambda_re': ssm_lambda_re, 'ssm_lambda_im': ssm_lambda_im, 'ssm_log_step': ssm_log_step,
        'ssm_b_re': ssm_b_re, 'ssm_b_im': ssm_b_im, 'ssm_c_re': ssm_c_re, 'ssm_c_im': ssm_c_im,
        'ssm_d': ssm_d, 'glu_w': glu_w, 'kv_ada_w': kv_ada_w, 'kv_ada_b': kv_ada_b, 'kv_norm_w': kv_norm_w,
        'kv_w': kv_w, 'k_norm_w': k_norm_w, 'attn_wq': attn_wq, 'q_norm_w': q_norm_w,
        'lambda_q1': lambda_q1, 'lambda_k1': lambda_k1, 'lambda_q2': lambda_q2, 'lambda_k2': lambda_k2,
        'subln_w': subln_w, 'attn_wo': attn_wo,
    }
    zeros = jnp.zeros((N_A_LAYERS, x_prompt.shape[0], N_GROUPS, STATE_DIM), jnp.float32)
    y_prompt, ssm_re_prompt, ssm_im_prompt, k_prompt, v_prompt = run_group(
        x_prompt, c_prompt, zeros, zeros, prompt_attend, p)
    n_pages = PAST_LEN // PAGE_SIZE
    db = x_sample.shape[0]
    k_past = cache_k[page_table].reshape(db, n_pages * PAGE_SIZE, N_HEADS, 2, HEAD_DIM)
    v_past = cache_v[page_table].reshape(db, n_pages * PAGE_SIZE, N_HEADS, V_DIM)
    y_sample, ssm_re_sample, ssm_im_sample, k_sample, v_sample = run_group(
        x_sample, c_sample, state_ssm_re, state_ssm_im, make_sample_attend(k_past, v_past), p)
    return (y_prompt, y_sample, ssm_re_prompt, ssm_im_prompt, k_prompt, v_prompt,
            ssm_re_sample, ssm_im_sample, k_sample, v_sample)
```

```python
import numpy as np
import ml_dtypes
import concourse.bass as bass
import concourse.mybir as mybir
from concourse.bass_utils import run_bass_kernel_spmd

F32 = mybir.dt.float32
BF16 = mybir.dt.bfloat16
I32 = mybir.dt.int32
AF = mybir.ActivationFunctionType
ALU = mybir.AluOpType
AX = mybir.AxisListType

D = 2048
NCT = 16
TT = 1024
G = 128
NST = 64
DEPTH = 4
NA = 2
DFF = 8192
NH = 8
EPS = 1e-6
SCALE = 128 ** -0.5
NPAGES = 128
GB = 16
TWO_PI = 2.0 * np.pi
DEBUG = False


class Prog:
    CE = ("pe", "act", "dve", "pool")
    NDS = 12

    def __init__(self):
        self.ins = []
        self.lastw = {}
        self.readers = {}
        self.dma_count = {"sp": 0, "pool": 0, "act": 0}
        self.floor = []

    def _add(self, eng, fn, r, w, kind, q=None):
        calls = []

        class _Rec:
            def __getattr__(self_, name):
                def f(*a, **kw):
                    calls.append((name, a, kw))
                    return None
                return f
        fn(_Rec())
        assert len(calls) == 1, calls
        _name, _a, _kw = calls[0]
        fn = (lambda e, _name=_name, _a=_a, _kw=_kw: getattr(e, _name)(*_a, **_kw))
        deps = set(self.floor)
        for k in list(r) + list(w):
            if k in self.lastw:
                deps.add(self.lastw[k])
        for k in w:
            deps.update(self.readers.get(k, ()))
        idx = len(self.ins)
        rec = dict(eng=eng, fn=fn, deps=deps, kind=kind, q=q)
        if kind == "dma":
            n = self.dma_count[q]
            self.dma_count[q] = n + 1
            rec["slot"] = n % self.NDS
            rec["gen"] = n // self.NDS
        self.ins.append(rec)
        for k in r:
            self.readers.setdefault(k, []).append(idx)
        for k in w:
            self.lastw[k] = idx
            self.readers[k] = []
        return idx

    def op(self, eng, fn, r=(), w=()):
        return self._add(eng, fn, r, w, "op")

    def dma(self, q, out, in_, r=(), w=(), **kw):
        eng = {"sp": "sp", "pool": "pool", "act": "act"}[q]
        return self._add(eng, lambda e: e.dma_start(out=out, in_=in_, **kw), r, w, "dma", q=q)

    def dmafn(self, q, fn, r=(), w=()):
        return self._add(q, fn, r, w, "dma", q=q)

    def barrier(self):
        last = {}
        for i, rec in enumerate(self.ins):
            if rec["kind"] == "dma":
                last[("dma", rec["q"], rec["slot"])] = i
            else:
                last[rec["eng"]] = i
        self.floor = list(last.values())

    def emit(self, nc, stack):
        ins = self.ins
        sems = {e: stack.enter_context(nc.semaphore("c_" + e)) for e in self.CE}
        dsems = {(q, s): stack.enter_context(nc.semaphore("d_%s_%d" % (q, s)))
                 for q in ("sp", "pool", "act") for s in range(self.NDS)}
        needed = set()
        for rec in ins:
            for d in rec["deps"]:
                needed.add(d)
        cnt = {e: 0 for e in self.CE}
        for i, rec in enumerate(ins):
            if rec["kind"] == "op":
                if i in needed:
                    cnt[rec["eng"]] += 1
                    rec["sig"] = cnt[rec["eng"]]
                else:
                    rec["sig"] = None
        per_eng = {e: [] for e in ("pe", "act", "dve", "pool", "sp")}
        for i, rec in enumerate(ins):
            per_eng[rec["eng"]].append(i)
        prev_on_slot = {}
        for i, rec in enumerate(ins):
            if rec["kind"] == "dma":
                key = (rec["q"], rec["slot"])
                rec["prev"] = prev_on_slot.get(key)
                prev_on_slot[key] = i
        block = stack.enter_context(nc.Block())

        def run(engname, e):
            waited_c = {x: 0 for x in self.CE}
            waited_d = {}
            for i in per_eng[engname]:
                rec = ins[i]
                deps = set(rec["deps"])
                if rec["kind"] == "dma" and rec["prev"] is not None:
                    deps.add(rec["prev"])
                cmax = {}
                for d in deps:
                    dr = ins[d]
                    if dr["kind"] == "dma":
                        key = (dr["q"], dr["slot"])
                        val = 16 * (dr["gen"] + 1)
                        if waited_d.get(key, 0) < val:
                            e.wait_ge(dsems[key], val)
                            waited_d[key] = val
                    else:
                        if dr["eng"] == "pe" and engname == "pe":
                            continue
                        cmax[dr["eng"]] = max(cmax.get(dr["eng"], 0), dr["sig"])
                for de, v in cmax.items():
                    if waited_c[de] < v:
                        e.wait_ge(sems[de], v)
                        waited_c[de] = v
                inst = rec["fn"](e)
                if rec["kind"] == "dma":
                    inst.then_inc(dsems[(rec["q"], rec["slot"])], 16)
                elif rec["sig"] is not None:
                    inst.then_inc(sems[rec["eng"]], 1)
            if engname in ("sp", "pool", "act"):
                for (q, s), i in prev_on_slot.items():
                    if q == engname:
                        val = 16 * (ins[i]["gen"] + 1)
                        if waited_d.get((q, s), 0) < val:
                            e.wait_ge(dsems[(q, s)], val)

        block.tensor(lambda e: run("pe", e))
        block.scalar(lambda e: run("act", e))
        block.vector(lambda e: run("dve", e))
        block.gpsimd(lambda e: run("pool", e))
        block.sync(lambda e: run("sp", e))


def build_program():
    from contextlib import ExitStack
    nc = bass.Bass("TRN2", target_bir_lowering=False)
    P = Prog()
    stack = ExitStack()

    def din(name, shape, dt=F32):
        return nc.dram_tensor(name, list(shape), dt, kind="ExternalInput").ap()

    def dout(name, shape, dt=F32):
        return nc.dram_tensor(name, list(shape), dt, kind="ExternalOutput").ap()

    def dscr(name, shape, dt=F32):
        if DEBUG:
            return nc.dram_tensor(name, list(shape), dt, kind="ExternalOutput").ap()
        return nc.dram_tensor(name, list(shape), dt).ap()

    xo = din("xo", [TT, D]); xp = din("xp", [TT, D]); xs = din("xs", [4, D])
    cvec = din("cvec", [2, D])
    s0re = din("s0re", [NA, G, NST]); s0im = din("s0im", [NA, G, NST])
    ptab = din("ptab", [1, NPAGES], I32)
    flag = din("flag", [128, 3])
    cache_k = din("cache_k", [1280 * 128, D]); cache_v = din("cache_v", [1280 * 128, D])
    ada_w = din("ada_w", [DEPTH, D, 6 * D]); ada_b = din("ada_b", [DEPTH, 6 * D])
    norm_w = din("norm_w", [DEPTH, 2, D])
    mlp_up = din("mlp_up", [DEPTH, D, DFF]); mlp_down = din("mlp_down", [DEPTH, DFF, D])
    lam_re = din("ssm_lambda_re", [NA, G, NST]); lam_im = din("ssm_lambda_im", [NA, G, NST])
    log_step = din("ssm_log_step", [NA, G])
    b_re = din("ssm_b_re", [NA, G, NST, 16]); b_im = din("ssm_b_im", [NA, G, NST, 16])
    c_re = din("ssm_c_re", [NA, G, 16, NST]); c_im = din("ssm_c_im", [NA, G, 16, NST])
    ssm_d = din("ssm_d", [NA, D]); glu_w = din("glu_w", [NA, D, 2 * D])
    kv_ada_w = din("kv_ada_w", [D, 2 * D]); kv_ada_b = din("kv_ada_b", [1, 2 * D])
    kv_norm_w = din("kv_norm_w", [1, D]); kv_w = din("kv_w", [D, 2 * D])
    k_norm_w = din("k_norm_w", [1, 128])
    attn_wq = din("attn_wq", [2, D, D]); q_norm_w = din("q_norm_w", [2, 128])
    lq1 = din("lambda_q1", [2, 128]); lk1 = din("lambda_k1", [2, 128])
    lq2 = din("lambda_q2", [2, 128]); lk2 = din("lambda_k2", [2, 128])
    subln_w = din("subln_w", [2, 256]); attn_wo = din("attn_wo", [2, D, D])
    cident = din("cident", [128, 128]); ctri = din("ctri", [128, 128]); ccaus = din("ccaus", [128, 128])
    y_o = dout("y_o", [TT, D]); y_s = dout("y_s", [4, D])
    sp_re = dout("sp_re", [NA, G, NST]); sp_im = dout("sp_im", [NA, G, NST])
    k_o = dout("k_o", [TT, D]); v_o = dout("v_o", [TT, D])
    ss_re = dout("ss_re", [NA, G, NST]); ss_im = dout("ss_im", [NA, G, NST])
    k_s = dout("k_s", [4, D]); v_s = dout("v_s", [4, D])
    modD = dscr("modD", [2, 4 * 6 * D + 2 * D])
    tabD = dscr("tabD", [NA, 6, G, 128, 128], BF16)
    ktD = dscr("ktD", [16, 128, 2 * TT], BF16)
    vD = dscr("vD", [2 * TT, D], BF16)
    sfinD = dscr("sfinD", [NA, 2, 128, G])

    sb = lambda name, shape, dt: stack.enter_context(nc.sbuf_tensor(name, list(shape), dt))
    X = sb("X", [128, 8, D], F32)
    XS = sb("XS", [128, D], F32)
    HT = sb("HT", [128, 9, D], BF16)
    HF = sb("HF", [128, NCT, TT + 4], BF16)
    WB = sb("WB", [128, 2, NCT, 512], BF16)
    HID = sb("HID", [128, 4, TT + 8], BF16)
    MV = sb("MV", [128, D], F32)
    IDB = sb("IDB", [128, 128], BF16)
    IDF = sb("IDF", [128, 128], F32)
    TRI = sb("TRI", [128, 128], BF16)
    CAUS = sb("CAUS", [128, 128], BF16)
    FLG = sb("FLG", [128, 3], F32)
    SM = sb("SM", [128, 64], F32)
    CT = sb("CT", [128, NCT, 2], BF16)
    PS = [stack.enter_context(nc.psum_tensor("ps%d" % i, [128, 512], F32)) for i in range(8)]

    def psbf(i):
        return PS[i][:].bitcast(BF16)

    with nc.allow_non_contiguous_dma(reason="small setup loads"):
        pass
    P.dma("sp", IDF[:], cident[:, :], w=["IDF"])
    P.dma("pool", TRI[:], ctri[:, :], w=["TRI"])
    P.dma("pool", CAUS[:], ccaus[:, :], w=["CAUS"])
    P.dma("sp", FLG[:], flag[:, :], w=["FLG"])
    P.op("dve", lambda e: e.tensor_copy(out=IDB[:], in_=IDF[:]), r=["IDF"], w=["IDB"])

    wslot = [0]

    def load_w(src_ap, kt, ncols):
        s = wslot[0]; wslot[0] ^= 1
        P.dma("pool", WB[:, s, 0:kt, 0:ncols], src_ap.rearrange("(k p) n -> p k n", p=128),
              w=[("WB", s)])
        return s

    def tok_cols(ti):
        if ti < 8:
            return HF[:, :, 0:TT].rearrange("p c (k j) -> p c k j", j=8)[:, :, :, ti]
        return HF[:, :, TT:TT + 4]

    def ntok(ti):
        return 128 if ti < 8 else 4

    def linear_tok(wsrc, ncols_total, tiles, evac, kt=NCT, src=None, blk=512):
        for c0 in range(0, ncols_total, blk):
            s = load_w(wsrc[:, c0:c0 + blk], kt, blk)
            for ti in tiles:
                bank = (ti + c0 // blk) % 4
                for k in range(kt):
                    lhs = tok_cols(ti)[:, k] if src is None else src(ti, k)
                    P.op("pe", (lambda e, lhs=lhs, k=k, s=s, bank=bank, ti=ti:
                                e.matmul(PS[bank][0:ntok(ti), 0:blk], lhsT=lhs, rhs=WB[:, s, k, 0:blk],
                                         start=(k == 0), stop=(k == kt - 1))),
                         r=[("WB", s), "HF"], w=[("PS", bank)])
                evac(ti, c0, blk, bank)

    def to_feature_major(ti, src_tile_ap, dstF=None, ncol=NCT):
        n = ntok(ti)
        for c4 in range(0, ncol, 8):
            bank = 4 + (c4 // 8) % 2
            pv = psbf(bank)
            for c in range(c4, min(c4 + 8, ncol)):
                P.op("pe", (lambda e, c=c, pv=pv, c4=c4:
                            e.transpose(out=pv[:, (c - c4) * 128:(c - c4) * 128 + n],
                                        in_=src_tile_ap[:, c * 128:(c + 1) * 128], identity=IDB[0:n, 0:n])),
                     r=["IDB", "HT"], w=[("PS", bank)])
            nc8 = min(8, ncol - c4)
            dst = (tok_cols(ti) if dstF is None else dstF)[:, c4:c4 + nc8]
            src_v = pv[:, 0:nc8 * 128].rearrange("p (c k) -> p c k", k=128)[:, :, 0:n]
            if (c4 // 8) % 2:
                P.op("act", (lambda e, dst=dst, src_v=src_v: e.copy(out=dst, in_=src_v)), r=[("PS", bank)], w=["HF"])
            else:
                P.op("dve", (lambda e, dst=dst, src_v=src_v: e.tensor_copy(out=dst, in_=src_v)), r=[("PS", bank)], w=["HF"])

    def bcast_row(dst, src_row_ap, key="MV"):
        P.dma("sp", dst, src_row_ap.partition_broadcast(128)[:, 0, :], w=[key])

    def xtile(ti):
        return X[:, ti, :] if ti < 8 else XS[0:4, :]

    def rms_modulate(tiles_rows, nw_row, sc_off, sh_off):
        for tiles, row in tiles_rows:
            bcast_row(MV[:, :], modD[row:row + 1, sc_off:sc_off + D])
            for hh in range(2):
                bcast_row(XTMP[:, :], nw_row[:, hh * 1024:(hh + 1) * 1024], key="XTMP")
                P.op("dve", (lambda e, hh=hh: e.scalar_tensor_tensor(
                    out=MV[:, hh * 1024:(hh + 1) * 1024], in0=MV[:, hh * 1024:(hh + 1) * 1024], scalar=1.0,
                    in1=XTMP[:, :], op0=ALU.add, op1=ALU.mult)), r=["XTMP"], w=["MV"])
            for ti in tiles:
                n = ntok(ti)
                xt = xtile(ti)
                P.op("dve", (lambda e, n=n, ti=ti: e.memset(SM[0:n, ti:ti + 1], 0.0)), r=[], w=["SM"])
                P.op("act", (lambda e, xt=xt, n=n, ti=ti:
                             e.activation(out=HT[0:n, ti, :], in_=xt, func=AF.Square,
                                          accum_out=SM[0:n, ti:ti + 1])),
                     r=["X"], w=["HT", "SM"])
                P.op("dve", (lambda e, n=n, ti=ti:
                             e.tensor_scalar(out=SM[0:n, ti:ti + 1], in0=SM[0:n, ti:ti + 1], scalar1=1.0 / D,
                                             scalar2=EPS, op0=ALU.mult, op1=ALU.add)), r=[], w=["SM"])
                P.op("act", (lambda e, n=n, ti=ti:
                             e.activation(out=SM[0:n, ti:ti + 1], in_=SM[0:n, ti:ti + 1], func=AF.Sqrt)),
                     r=[], w=["SM"])
                P.op("dve", (lambda e, n=n, ti=ti:
                             e.reciprocal(out=SM[0:n, ti:ti + 1], in_=SM[0:n, ti:ti + 1])), r=[], w=["SM"])
                P.op("dve", (lambda e, xt=xt, n=n, ti=ti:
                             e.scalar_tensor_tensor(out=HT[0:n, ti, :], in0=xt, scalar=SM[0:n, ti:ti + 1],
                                                    in1=MV[0:n, :], op0=ALU.mult, op1=ALU.mult)),
                     r=["X", "MV", "SM"], w=["HT"])
            bcast_row(MV[:, :], modD[row:row + 1, sh_off:sh_off + D])
            for ti in tiles:
                n = ntok(ti)
                P.op("pool", (lambda e, n=n, ti=ti:
                              e.tensor_tensor(out=HT[0:n, ti, :], in0=HT[0:n, ti, :], in1=MV[0:n, :],
                                              op=ALU.add)), r=["MV"], w=["HT"])

    XTMP = sb("XTMP", [128, 1024], F32)
    XB16 = XTMP[:, :].bitcast(BF16).rearrange("p (g j c) -> p g j c", j=8, c=16)

    coefD = dscr("coefD", [NA, 128, 6 * 128])
    dkD = dscr("dkD", [NA, 128, 128])
    SFIN = sb("SFIN", [128, 2, 128], F32)
    SINIT = SFIN
    S0T = sb("S0T", [128, 2, 128], F32)
    SSF = S0T
    SCUR = sb("SCUR", [128, 2, 2, 32], F32)
    STMP = sb("STMP", [128, 4, 32], F32)
    USAMP = sb("USAMP", [128, 128], BF16)
    YS = sb("YS", [128, 128], BF16)
    MT = XTMP[0:2, 0:512]
    MB = XTMP[0:2, 512:1024]
    hsD = dscr("hsD", [4, D], BF16)
    ysD = dscr("ysD", [4, D], BF16)

    for r_ in range(2):
        P.dma("pool", CT[:, :, r_], cvec[r_:r_ + 1, :].rearrange("o (c p) -> p (o c)", p=128), w=["CT"],
              allow_slow_non_contiguous=True)
    mod_specs = [(ada_w[l], ada_b[l:l + 1, :], 6 * D, l * 6 * D) for l in range(DEPTH)]
    mod_specs.append((kv_ada_w, kv_ada_b, 2 * D, 24 * D))
    for wsrc, bsrc, ncol, off in mod_specs:
        for c0 in range(0, ncol, 512):
            s = load_w(wsrc[:, c0:c0 + 512], NCT, 512)
            P.dma("sp", MB, bsrc[:, c0:c0 + 512].partition_broadcast(2)[:, 0, :], w=["MB"])
            for k in range(NCT):
                P.op("pe", (lambda e, k=k, s=s: e.matmul(PS[0][0:2, 0:512], lhsT=CT[:, k, :], rhs=WB[:, s, k, :],
                                                         start=(k == 0), stop=(k == NCT - 1))),
                     r=[("WB", s), "CT"], w=[("PS", 0)])
            P.op("dve", lambda e: e.tensor_tensor(out=MT, in0=PS[0][0:2, 0:512], in1=MB, op=ALU.add),
                 r=[("PS", 0), "MB"], w=["MT"])
            P.dma("sp", modD[:, off + c0:off + c0 + 512], MT, r=["MT"], w=["modD"])
    P.barrier()

    def xs_(i, a, b):
        return X[:, i, a:b]
    BR = X[:, 0, 0:1024].rearrange("p (n c) -> p n c", c=16); BI = X[:, 0, 1024:2048].rearrange("p (n c) -> p n c", c=16)
    CR = X[:, 1, 0:1024].rearrange("p (c n) -> p c n", n=64); CI = X[:, 1, 1024:2048].rearrange("p (c n) -> p c n", n=64)
    BbR = X[:, 2, 0:1024].rearrange("p (n c) -> p n c", c=16); BbI = X[:, 2, 1024:2048].rearrange("p (n c) -> p n c", c=16)
    ANG = X[:, 3, 0:1024]; RR = X[:, 3, 1024:2048]
    MAG = X[:, 4, 0:1024].rearrange("p (d n) -> p d n", n=64); RI = X[:, 4, 1024:2048].bitcast(I32)
    PR = X[:, 5, 0:1024].rearrange("p (d n) -> p d n", n=64); PIm = X[:, 5, 1024:2048].rearrange("p (d n) -> p d n", n=64)
    T1 = X[:, 6, 0:1024].rearrange("p (n c) -> p n c", c=16); T2 = X[:, 6, 1024:2048].rearrange("p (n c) -> p n c", c=16)
    LR = X[:, 7, 0:64]; LI = X[:, 7, 64:128]; LS = X[:, 7, 128:129]; DT = X[:, 7, 129:130]
    LRDT = X[:, 7, 192:256]; TH = X[:, 7, 256:320]; FRE = X[:, 7, 320:384]; FIM = X[:, 7, 384:448]
    DEN = X[:, 7, 448:512]; XR = X[:, 7, 512:576]; TA = X[:, 7, 576:640]; TB_ = X[:, 7, 640:704]
    TR2 = X[:, 7, 768:1024]
    TABT = HT[:, 0:8, :]
    K = "PREP"

    def dve(fn, r=(K,), w=(K,)):
        P.op("dve", fn, r=list(r), w=list(w))

    def didx(d):
        return d + 7

    for l in range(NA):
        P.dma("sp", LR, lam_re[l], w=[K]); P.dma("sp", LI, lam_im[l], w=[K])
        P.dma("sp", LS, log_step[l:l + 1, :].rearrange("o g -> g o"), w=[K], allow_slow_non_contiguous=True)
        P.dma("sp", BR, b_re[l], w=[K]); P.dma("sp", BI, b_im[l], w=[K])
        P.dma("sp", CR, c_re[l], w=[K]); P.dma("sp", CI, c_im[l], w=[K])
        P.op("act", lambda e: e.activation(out=DT, in_=LS, func=AF.Exp), r=[K], w=[K])
        dve(lambda e: e.tensor_scalar(out=LRDT, in0=LR, scalar1=DT, scalar2=None, op0=ALU.mult))
        dve(lambda e: e.tensor_scalar(out=TH, in0=LI, scalar1=DT, scalar2=None, op0=ALU.mult))
        for di in range(16):
            d = float(di - 7)
            P.op("act", (lambda e, di=di, d=d: e.activation(out=MAG[:, di, :], in_=LRDT, func=AF.Exp, scale=d)),
                 r=[K], w=[K])
            dve(lambda e, di=di, d=d: e.tensor_scalar(out=ANG[:, di * 64:(di + 1) * 64], in0=TH,
                                                      scalar1=d / TWO_PI, scalar2=None, op0=ALU.mult))
        for which, dst in ((0.0, PIm), (0.25, PR)):
            dve(lambda e, which=which: e.tensor_scalar(out=RR, in0=ANG, scalar1=which, scalar2=None, op0=ALU.add))
            dve(lambda e: e.tensor_copy(out=RI, in_=RR))
            dve(lambda e: e.tensor_copy(out=X[:, 6, 0:1024], in_=RI))
            dve(lambda e: e.tensor_tensor(out=RR, in0=RR, in1=X[:, 6, 0:1024], op=ALU.subtract))
            dve(lambda e: e.tensor_scalar(out=X[:, 6, 0:1024], in0=RR, scalar1=0.5, scalar2=None, op0=ALU.is_gt))
            dve(lambda e: e.tensor_tensor(out=RR, in0=RR, in1=X[:, 6, 0:1024], op=ALU.subtract))
            dve(lambda e: e.tensor_scalar(out=X[:, 6, 0:1024], in0=RR, scalar1=-0.5, scalar2=None, op0=ALU.is_lt))
            dve(lambda e: e.tensor_tensor(out=RR, in0=RR, in1=X[:, 6, 0:1024], op=ALU.add))
            P.op("act", (lambda e, dst=dst: e.activation(out=dst.rearrange("p d n -> p (d n)"), in_=RR, func=AF.Sin,
                                                         scale=TWO_PI)), r=[K], w=[K])
            dve(lambda e, dst=dst: e.tensor_tensor(out=dst, in0=dst, in1=MAG, op=ALU.mult))
        a_re = PR[:, didx(1), :]; a_im = PIm[:, didx(1), :]
        dve(lambda e: e.tensor_scalar(out=XR, in0=a_re, scalar1=-1.0, scalar2=None, op0=ALU.add))
        dve(lambda e: e.tensor_tensor(out=DEN, in0=LR, in1=LR, op=ALU.mult))
        dve(lambda e: e.tensor_tensor(out=TA, in0=LI, in1=LI, op=ALU.mult))
        dve(lambda e: e.tensor_tensor(out=DEN, in0=DEN, in1=TA, op=ALU.add))
        dve(lambda e: e.reciprocal(out=DEN, in_=DEN))
        dve(lambda e: e.tensor_tensor(out=TA, in0=XR, in1=LR, op=ALU.mult))
        dve(lambda e: e.tensor_tensor(out=TB_, in0=a_im, in1=LI, op=ALU.mult))
        dve(lambda e: e.tensor_tensor(out=TA, in0=TA, in1=TB_, op=ALU.add))
        dve(lambda e: e.tensor_tensor(out=FRE, in0=TA, in1=DEN, op=ALU.mult))
        dve(lambda e: e.tensor_tensor(out=TA, in0=a_im, in1=LR, op=ALU.mult))
        dve(lambda e: e.tensor_tensor(out=TB_, in0=XR, in1=LI, op=ALU.mult))
        dve(lambda e: e.tensor_tensor(out=TA, in0=TA, in1=TB_, op=ALU.subtract))
        dve(lambda e: e.tensor_tensor(out=FIM, in0=TA, in1=DEN, op=ALU.mult))
        fre_b = FRE.unsqueeze(2).broadcast_to([128, 64, 16]); fim_b = FIM.unsqueeze(2).broadcast_to([128, 64, 16])
        dve(lambda e: e.tensor_tensor(out=T1, in0=BR, in1=fre_b, op=ALU.mult))
        dve(lambda e: e.tensor_tensor(out=T2, in0=BI, in1=fim_b, op=ALU.mult))
        dve(lambda e: e.tensor_tensor(out=BbR, in0=T1, in1=T2, op=ALU.subtract))
        dve(lambda e: e.tensor_tensor(out=T1, in0=BI, in1=fre_b, op=ALU.mult))
        dve(lambda e: e.tensor_tensor(out=T2, in0=BR, in1=fim_b, op=ALU.mult))
        dve(lambda e: e.tensor_tensor(out=BbI, in0=T1, in1=T2, op=ALU.add))
        CRn = CR.rearrange("p c n -> p n c"); CIn = CI.rearrange("p c n -> p n c")
        kinds = [(BbR, BbI, lambda j: -j, 0, False, False),
                 (CRn, CIn, lambda j: j, 0, False, True),
                 (CRn, CIn, lambda j: j + 1, 0, False, True),
                 (BbR, BbI, lambda j: 7 - j, 1, False, False),
                 (BbR, BbI, lambda j: 7 - j, 1, True, False)]
        for kind, (sR, sI, dj, layout, swap, negim) in enumerate(kinds):
            if layout == 0:
                TV = TABT.rearrange("p a b -> p (a b)").rearrange("p (m j c) -> p m j c", j=8, c=16)
            else:
                TV = TABT.rearrange("p a b -> p (a b)").rearrange("p (j c m) -> p j c m", c=16, m=128)
            for j in range(8):
                di = didx(dj(j))
                prb = PR[:, di, :].unsqueeze(2).broadcast_to([128, 64, 16])
                pib = PIm[:, di, :].unsqueeze(2).broadcast_to([128, 64, 16])
                for h in range(2):
                    hh = (1 - h) if swap else h
                    if layout == 0:
                        outv = TV[:, hh * 64:(hh + 1) * 64, j, :]
                    else:
                        outv = TV[:, j, :, hh * 64:(hh + 1) * 64].rearrange("p c n -> p n c")
                    if not negim:
                        a0, b0, opx = (prb, pib, ALU.subtract) if h == 0 else (pib, prb, ALU.add)
                        dve(lambda e, a0=a0: e.tensor_tensor(out=T1, in0=sR, in1=a0, op=ALU.mult))
                        dve(lambda e, b0=b0: e.tensor_tensor(out=T2, in0=sI, in1=b0, op=ALU.mult))
                        P.op("pool", (lambda e, outv=outv, opx=opx: e.tensor_tensor(out=outv, in0=T1, in1=T2, op=opx)),
                             r=[K], w=[K, "HT"])
                    else:
                        if h == 0:
                            dve(lambda e: e.tensor_tensor(out=T1, in0=sR, in1=prb, op=ALU.mult))
                            dve(lambda e: e.tensor_tensor(out=T2, in0=sI, in1=pib, op=ALU.mult))
                            P.op("pool", (lambda e, outv=outv: e.tensor_tensor(out=outv, in0=T1, in1=T2, op=ALU.subtract)),
                                 r=[K], w=[K, "HT"])
                        else:
                            dve(lambda e: e.tensor_tensor(out=T1, in0=sR, in1=pib, op=ALU.mult))
                            dve(lambda e: e.tensor_tensor(out=T2, in0=sI, in1=prb, op=ALU.mult))
                            dve(lambda e: e.tensor_tensor(out=T1, in0=T1, in1=T2, op=ALU.add))
                            P.op("pool", (lambda e, outv=outv: e.tensor_scalar(out=outv, in0=T1, scalar1=-1.0, scalar2=None,
                                                                               op0=ALU.mult)), r=[K], w=[K, "HT"])
            P.dma("sp", tabD[l, kind].rearrange("g a b -> g (a b)"), TABT.rearrange("p a b -> p (a b)"),
                  r=[K, "HT"], w=["tabD"])
        TR2v = TR2
        specs = [(8, 1.0, 1.0, PR), (8, -1.0, 1.0, PIm), (8, 1.0, -1.0, PIm),
                 (-4, 1.0, 1.0, PR), (-4, -1.0, 1.0, PIm), (-4, 1.0, -1.0, PIm)]
        for ci, (dd, s0, s1, srcT) in enumerate(specs):
            dve(lambda e, dd=dd, s0=s0, srcT=srcT: e.tensor_scalar(out=TR2v[:, 0:64], in0=srcT[:, didx(dd), :], scalar1=s0,
                                                                    scalar2=None, op0=ALU.mult))
            dve(lambda e, dd=dd, s1=s1, srcT=srcT: e.tensor_scalar(out=TR2v[:, 64:128], in0=srcT[:, didx(dd), :], scalar1=s1,
                                                                    scalar2=None, op0=ALU.mult))
            P.op("pe", lambda e: e.transpose(out=PS[7][:, 0:128], in_=TR2v[:, 0:128], identity=IDF[:]),
                 r=[K, "IDF"], w=[("PS", 7)])
            P.op("act", (lambda e, ci=ci: e.copy(out=X[:, 7, 1024 + ci * 128:1024 + (ci + 1) * 128], in_=PS[7][:, 0:128])),
                 r=[("PS", 7)], w=[K])
        P.dma("sp", coefD[l], X[:, 7, 1024:1792], r=[K], w=["coefD"])
        P.dma("sp", X[0:16, 7, 1920:2048], ssm_d[l:l + 1, :].rearrange("o (g c) -> c (o g)", c=16), w=[K],
              allow_slow_non_contiguous=True)
        for j in range(8):
            P.dma("sp", dkD[l, j * 16:(j + 1) * 16, :], X[0:16, 7, 1920:2048], r=[K], w=["dkD"])
    P.barrier()
    GSV = sb("GSV", [4, D], F32)
    WBf = WB[:].rearrange("p s k n -> p (s k n)")
    DDv = WBf[:, 0:4128].rearrange("p (g k) -> p g k", k=129)
    DSv = WBf[:, 4128:8256].rearrange("p (g k) -> p g k", k=129)
    SAv = WBf[:, 8256:12384].rearrange("p (g k) -> p g k", k=129)
    TB2 = WBf[:, 12384:13664].rearrange("p (s t c) -> p s t c", s=2, t=5)
    TSB = WBf[:, 13664:13792]
    YSB = WBf[:, 13792:13922]
    COEFL = WBf[:, 13924:15460].bitcast(F32).rearrange("p (a g) -> p a g", g=128)
    DKL = WBf[:, 15460:15716].bitcast(F32)
    UAv = HID[:].rearrange("p a b -> p (a b)")[:, 0:4128].rearrange("p (g k) -> p g k", k=129)
    P.op("pool", lambda e: e.memset(USAMP[:, :], 0.0), w=["USAMP"])
    P.op("pool", lambda e: e.memset(SFIN[:, :, :], 0.0), w=["SFIN"])
    dbg = {}

    def dbg_dump(name, src_ap, shape, dt=F32, keys=()):
        if not DEBUG:
            return
        o = dout("dbg_" + name, shape, dt)
        P.dma("sp", o, src_ap, r=list(keys))
        dbg[name] = o

    def load_x(src, with_sample):
        sv = src.rearrange("(k j) d -> k j d", j=8)
        for j in range(8):
            P.dma("sp", X[:, j, :], sv[:, j, :], w=["X"])
        if with_sample:
            P.dma("sp", XS[0:4, :], xs[:, :], w=["X"])

    def ssm_pass(l, own):
        tiles = list(range(8))
        if own:
            P.op("pool", lambda e: e.memset(USAMP[:, :], 0.0), w=["USAMP"])
            P.dma("sp", hsD[:, :], HT[0:4, 8, :], r=["HT"], w=["hsD"])
            for j in range(4):
                P.dma("sp", USAMP[j * 16:(j + 1) * 16, :], hsD[j:j + 1, :].rearrange("o (g c) -> c (o g)", c=16),
                      r=["hsD"], w=["USAMP"], allow_slow_non_contiguous=True)
            P.dma("sp", S0T[0:64, 0, :], s0re[l].rearrange("g n -> n g"), w=["S0T"], allow_slow_non_contiguous=True)
            P.dma("sp", S0T[64:128, 0, :], s0im[l].rearrange("g n -> n g"), w=["S0T"], allow_slow_non_contiguous=True)
            P.dma("sp", S0T[0:64, 1, :], s0im[l].rearrange("g n -> n g"), w=["S0T"], allow_slow_non_contiguous=True)
            P.dma("sp", S0T[64:128, 1, :], s0re[l].rearrange("g n -> n g"), w=["S0T"], allow_slow_non_contiguous=True)
            P.dma("sp", SFIN[:, :, :], sfinD[l].rearrange("s p g -> p s g"), r=["sfinD"], w=["SFIN"])
            P.op("dve", lambda e: e.tensor_scalar(out=SINIT[:, :, :], in0=SFIN[:, :, :], scalar1=FLG[:, 0:1], scalar2=None,
                                                  op0=ALU.mult), r=["SFIN", "FLG"], w=["SINIT", "SFIN"])
        else:
            P.op("dve", lambda e: e.memset(SINIT[:, :, :], 0.0), w=["SINIT", "SFIN"])
        P.dma("sp", COEFL, coefD[l].rearrange("p (a g) -> p a g", g=128), w=["COEFL"])
        P.dma("sp", DKL, dkD[l], w=["DKL"])
        P.barrier()
        A8 = COEFL[:, 0, :]; Bc8 = COEFL[:, 1, :]; Bcs8 = COEFL[:, 2, :]
        A4 = COEFL[:, 3, :]; Bc4 = COEFL[:, 4, :]
        for rnd in range(4):
            g0 = rnd * 32
            for gi in range(32):
                g = g0 + gi
                slot = gi % 2
                P.dma("sp", TB2[:, slot, :, :], tabD[l, 0:5, g].rearrange("t p c -> p t c"), w=[("TB2", slot)])
                ub = 4 + gi % 2
                if gi % 16 == 0:
                    P.op("pool", (lambda e, g=g: e.tensor_copy(
                        out=XB16, in_=HT[:, 0:8, 16 * g:16 * g + 256].rearrange("p j (g c) -> p g j c", c=16))),
                        r=["HT"], w=["XTMP"])
                P.op("pe", (lambda e, gi=gi, ub=ub: e.transpose(out=psbf(ub)[:, 0:128],
                                                              in_=XB16[:, gi % 16].rearrange("p j c -> p (j c)"),
                                                              identity=IDB[:])), r=["XTMP", "IDB"], w=[("PS", ub)])
                P.op("act", (lambda e, gi=gi, ub=ub: e.copy(out=UAv[:, gi, 0:128], in_=psbf(ub)[:, 0:128])),
                     r=[("PS", ub)], w=[("UA", gi)])
                if own:
                    P.op("pool", (lambda e, gi=gi, g=g: e.tensor_copy(out=UAv[:, gi, 128:129], in_=USAMP[:, g:g + 1])),
                         r=["USAMP"], w=[("UA", gi)])
                else:
                    P.op("pool", (lambda e, gi=gi: e.memset(UAv[:, gi, 128:129], 0.0)), w=[("UA", gi)])
                b0 = gi % 2; b1 = 2 + gi % 2
                P.op("pe", (lambda e, gi=gi, slot=slot, b0=b0: e.matmul(PS[b0][:, 0:129], lhsT=TB2[:, slot, 3, :], rhs=UAv[:, gi, :],
                                                                        start=True, stop=True)),
                     r=[("TB2", slot), ("UA", gi)], w=[("PS", b0)])
                P.op("pe", (lambda e, gi=gi, slot=slot, b1=b1: e.matmul(PS[b1][:, 0:129], lhsT=TB2[:, slot, 4, :], rhs=UAv[:, gi, :],
                                                                        start=True, stop=True)),
                     r=[("TB2", slot), ("UA", gi)], w=[("PS", b1)])
                P.op("dve", (lambda e, gi=gi, b0=b0: e.tensor_copy(out=DDv[:, gi, :], in_=PS[b0][:, 0:129])),
                     r=[("PS", b0)], w=["DD"])
                P.op("act", (lambda e, gi=gi, b1=b1: e.copy(out=DSv[:, gi, :], in_=PS[b1][:, 0:129])),
                     r=[("PS", b1)], w=["DS"])
            P.barrier()
            EN = "dve"
            a8 = A8[:, g0:g0 + 32]; bc8 = Bc8[:, g0:g0 + 32]; bcs8 = Bcs8[:, g0:g0 + 32]
            P.op(EN, (lambda e, g0=g0: e.tensor_copy(out=SCUR[:, 0, :, :], in_=SINIT[:, :, g0:g0 + 32])),
                 r=["SINIT"], w=["SC"])
            for k in range(128):
                pa = k % 2; pb = 1 - pa
                S = SCUR[:, pa, 0, :]; Sw = SCUR[:, pa, 1, :]
                P.op("act", (lambda e, k=k, S=S: e.copy(out=SAv[:, :, k], in_=S)), r=["SC"], w=["SA"])
                P.op("dve", (lambda e, S=S: e.tensor_tensor(out=STMP[:, 0, :], in0=S, in1=a8, op=ALU.mult)), r=["SC"], w=["ST0"])
                P.op("pool", (lambda e, Sw=Sw: e.tensor_tensor(out=STMP[:, 1, :], in0=Sw, in1=bc8, op=ALU.mult)), r=["SC"], w=["ST1"])
                P.op("dve", (lambda e, Sw=Sw: e.tensor_tensor(out=STMP[:, 2, :], in0=Sw, in1=a8, op=ALU.mult)), r=["SC"], w=["ST2"])
                P.op("pool", (lambda e, S=S: e.tensor_tensor(out=STMP[:, 3, :], in0=S, in1=bcs8, op=ALU.mult)), r=["SC"], w=["ST3"])
                P.op("dve", (lambda e, k=k: e.tensor_tensor(out=STMP[:, 0, :], in0=STMP[:, 0, :], in1=DDv[:, :, k], op=ALU.add)),
                     r=["DD"], w=["ST0"])
                P.op("pool", (lambda e, k=k: e.tensor_tensor(out=STMP[:, 3, :], in0=STMP[:, 3, :], in1=DSv[:, :, k], op=ALU.add)),
                     r=["DS"], w=["ST3"])
                P.op("dve", (lambda e, pb=pb: e.tensor_tensor(out=SCUR[:, pb, 0, :], in0=STMP[:, 0, :], in1=STMP[:, 1, :], op=ALU.add)),
                     r=["ST0", "ST1"], w=["SC"])
                P.op("pool", (lambda e, pb=pb: e.tensor_tensor(out=SCUR[:, pb, 1, :], in0=STMP[:, 2, :], in1=STMP[:, 3, :], op=ALU.add)),
                     r=["ST2", "ST3"], w=["SC"])
            P.op("dve", (lambda e, g0=g0: e.tensor_copy(out=SFIN[:, :, g0:g0 + 32], in_=SCUR[:, 0, :, :])), r=["SC"], w=["SFIN"])
            if own:
                a4 = A4[:, g0:g0 + 32]; bc4 = Bc4[:, g0:g0 + 32]
                s0 = S0T[:, 0, g0:g0 + 32]; s0w = S0T[:, 1, g0:g0 + 32]
                P.op("act", (lambda e, s0=s0: e.copy(out=SAv[:, :, 128], in_=s0)), r=["S0T"], w=["SA"])
                P.op("dve", (lambda e, s0=s0: e.tensor_tensor(out=STMP[:, 0, :], in0=s0, in1=a8, op=ALU.mult)), r=["S0T"], w=["ST0"])
                P.op("dve", (lambda e, s0w=s0w: e.tensor_tensor(out=STMP[:, 1, :], in0=s0w, in1=bc8, op=ALU.mult)), r=["S0T"], w=["ST1"])
                P.op("dve", lambda e: e.tensor_tensor(out=STMP[:, 0, :], in0=STMP[:, 0, :], in1=STMP[:, 1, :], op=ALU.add), r=["ST1"], w=["ST0"])
                P.op("dve", lambda e: e.tensor_tensor(out=STMP[:, 0, :], in0=STMP[:, 0, :], in1=DDv[:, :, 128], op=ALU.add), r=["DD"], w=["ST0"])
                P.op("dve", (lambda e, s0w=s0w: e.tensor_tensor(out=STMP[:, 2, :], in0=s0w, in1=a8, op=ALU.mult)), r=["S0T"], w=["ST2"])
                P.op("dve", (lambda e, s0=s0: e.tensor_tensor(out=STMP[:, 3, :], in0=s0, in1=bcs8, op=ALU.mult)), r=["S0T"], w=["ST3"])
                P.op("dve", lambda e: e.tensor_tensor(out=STMP[:, 2, :], in0=STMP[:, 2, :], in1=STMP[:, 3, :], op=ALU.add), r=["ST3"], w=["ST2"])
                P.op("dve", lambda e: e.tensor_tensor(out=STMP[:, 2, :], in0=STMP[:, 2, :], in1=DSv[:, :, 128], op=ALU.add), r=["DS"], w=["ST2"])
                P.op("dve", lambda e: e.tensor_tensor(out=STMP[:, 0, :], in0=STMP[:, 0, :], in1=a4, op=ALU.mult), r=[], w=["ST0"])
                P.op("dve", lambda e: e.tensor_tensor(out=STMP[:, 2, :], in0=STMP[:, 2, :], in1=bc4, op=ALU.mult), r=[], w=["ST2"])
                P.op("dve", (lambda e, g0=g0: e.tensor_tensor(out=SSF[:, 0, g0:g0 + 32], in0=STMP[:, 0, :], in1=STMP[:, 2, :], op=ALU.add)),
                     r=["ST0", "ST2"], w=["SSF"])
            P.barrier()
            for gi in range(32):
                g = g0 + gi
                slot = gi % 2
                P.dma("sp", TB2[:, slot, 0:3, :], tabD[l, 0:3, g].rearrange("t p c -> p t c"), w=[("TB2", slot)])
                P.op("pe", (lambda e, slot=slot: e.matmul(PS[6][:, 0:128], lhsT=TB2[:, slot, 0, :], rhs=TB2[:, slot, 1, :],
                                                          start=True, stop=True)), r=[("TB2", slot)], w=[("PS", 6)])
                P.op("dve", lambda e: e.tensor_tensor(out=TSB, in0=PS[6][:, 0:128], in1=TRI[:, :], op=ALU.mult),
                     r=[("PS", 6), "TRI"], w=["TSB"])
                P.op("dve", (lambda e, g=g: e.scalar_tensor_tensor(out=TSB, in0=IDF[:, :], scalar=DKL[:, g:g + 1], in1=TSB,
                                                                   op0=ALU.mult, op1=ALU.add)), r=["IDF"], w=["TSB"])
                P.op("pe", (lambda e, gi=gi: e.matmul(PS[7][:, 0:129], lhsT=TSB, rhs=UAv[:, gi, :], start=True, stop=False)),
                     r=["TSB", ("UA", gi)], w=[("PS", 7)])
                P.op("pe", (lambda e, gi=gi, slot=slot: e.matmul(PS[7][:, 0:129], lhsT=TB2[:, slot, 2, :], rhs=SAv[:, gi, :],
                                                                 start=False, stop=True)), r=[("TB2", slot), "SA"], w=[("PS", 7)])
                P.op("act", lambda e: e.copy(out=YSB[:, 0:129], in_=PS[7][:, 0:129]), r=[("PS", 7)], w=["YSB"])
                yb = 4 + gi % 2
                P.op("pe", (lambda e, yb=yb: e.transpose(out=psbf(yb)[:, 0:128], in_=YSB[:, 0:128], identity=IDB[:])),
                     r=["YSB", "IDB"], w=[("PS", yb)])
                P.op("dve", (lambda e, yb=yb, g=g: e.tensor_copy(out=HT[:, 0:8, 16 * g:16 * g + 16],
                                                                 in_=psbf(yb)[:, 0:128].rearrange("p (j c) -> p j c", c=16))),
                     r=[("PS", yb)], w=["HT"])
                if own:
                    P.op("pool", (lambda e, g=g: e.tensor_copy(out=YS[:, g:g + 1], in_=YSB[:, 128:129])), r=["YSB"], w=["YS"])
            P.barrier()
        if own:
            for j in range(4):
                P.dma("sp", ysD[j:j + 1, :].rearrange("o (g c) -> c (o g)", c=16), YS[j * 16:(j + 1) * 16, :],
                      r=["YS"], w=["ysD"], allow_slow_non_contiguous=True)
            P.dma("sp", HT[0:4, 8, :], ysD[:, :], r=["ysD"], w=["HT"])
            P.dma("sp", sp_re[l].rearrange("g n -> n g"), SFIN[0:64, 0, :], r=["SFIN"], allow_slow_non_contiguous=True)
            P.dma("sp", sp_im[l].rearrange("g n -> n g"), SFIN[64:128, 0, :], r=["SFIN"], allow_slow_non_contiguous=True)
            P.dma("sp", ss_re[l].rearrange("g n -> n g"), SSF[0:64, 0, :], r=["SSF"], allow_slow_non_contiguous=True)
            P.dma("sp", ss_im[l].rearrange("g n -> n g"), SSF[64:128, 0, :], r=["SSF"], allow_slow_non_contiguous=True)
        else:
            P.dma("sp", sfinD[l].rearrange("s p g -> p s g"), SFIN[:, :, :], r=["SFIN"], w=["sfinD"])
        P.barrier()

    def hid_cols(ti):
        if ti < 8:
            return HID[:, :, 0:TT].rearrange("p c (k j) -> p c k j", j=8)[:, :, :, ti]
        return HID[:, :, TT:TT + 4]

    def gate_vec(ti, c0, n):
        return MV[0:128, c0:c0 + n] if ti < 8 else GSV[0:4, c0:c0 + n]

    def load_gate(l, q, own):
        bcast_row(MV[:, :], modD[0:1, l * 6 * D + q * D:l * 6 * D + (q + 1) * D])
        if own:
            P.dma("sp", GSV[:, :], modD[1:2, l * 6 * D + q * D:l * 6 * D + (q + 1) * D].partition_broadcast(4)[:, 0, :], w=["GSV"])

    xh = [0]

    def resid_add(ti, c0, n, bank):
        nt = ntok(ti)
        h_ = xh[0]; xh[0] ^= 1
        tmp = XTMP[0:nt, h_ * 512:h_ * 512 + n]
        P.op("dve", (lambda e, tmp=tmp, nt=nt, ti=ti: e.tensor_tensor(out=tmp, in0=PS[bank][0:nt, 0:n], in1=gate_vec(ti, c0, n),
                                                                    op=ALU.mult)), r=[("PS", bank), "MV", "GSV"], w=[("XT", h_)])
        xt = xtile(ti)
        P.op("pool", (lambda e, tmp=tmp, xt=xt: e.tensor_tensor(out=xt[:, c0:c0 + n], in0=xt[:, c0:c0 + n], in1=tmp, op=ALU.add)),
             r=[("XT", h_)], w=["X"])

    def a_layer(l, own):
        tiles = list(range(8)) + ([8] if own else [])
        rows = [(list(range(8)), 0)] + ([([8], 1)] if own else [])
        base = l * 6 * D
        rms_modulate(rows, norm_w[l, 0:1, :], base + D, base)
        P.barrier()
        if l == 0 and own:
            for j in range(2):
                dbg_dump("h0_%d" % j, HT[:, j, :], [128, D], BF16, keys=["HT"])
        ssm_pass(l, own)
        if l == 0 and own:
            for j in range(2):
                dbg_dump("y0_%d" % j, HT[:, j, :], [128, D], BF16, keys=["HT"])
            dbg_dump("ys0", HT[0:4, 8, :], [4, D], BF16, keys=["HT"])
        for ti in tiles:
            n = ntok(ti)
            P.op("act", (lambda e, n=n, ti=ti: e.activation(out=HT[0:n, ti, :], in_=HT[0:n, ti, :], func=AF.Gelu_apprx_tanh)),
                 r=[], w=["HT"])
        for ti in tiles:
            to_feature_major(ti, HT[0:ntok(ti), ti, :])
        P.barrier()
        load_gate(l, 2, own)
        for c0 in range(0, D, 512):
            sA = load_w(glu_w[l][:, c0:c0 + 512], NCT, 512)
            sB = load_w(glu_w[l][:, D + c0:D + c0 + 512], NCT, 512)
            for ti in tiles:
                n = ntok(ti)
                bv = (2 * ti) % 4; bg = bv + 1
                for (s, bank) in ((sA, bv), (sB, bg)):
                    for k in range(NCT):
                        P.op("pe", (lambda e, k=k, s=s, bank=bank, ti=ti, n=n:
                                    e.matmul(PS[bank][0:n, 0:512], lhsT=tok_cols(ti)[:, k], rhs=WB[:, s, k, :],
                                             start=(k == 0), stop=(k == NCT - 1))), r=[("WB", s), "HF"], w=[("PS", bank)])
                h_ = xh[0]; xh[0] ^= 1
                tmp = XTMP[0:n, h_ * 512:(h_ + 1) * 512]
                P.op("act", (lambda e, tmp=tmp, bg=bg, n=n: e.activation(out=tmp, in_=PS[bg][0:n, 0:512], func=AF.Sigmoid)),
                     r=[("PS", bg)], w=[("XT", h_)])
                P.op("dve", (lambda e, tmp=tmp, bv=bv, n=n: e.tensor_tensor(out=tmp, in0=tmp, in1=PS[bv][0:n, 0:512], op=ALU.mult)),
                     r=[("PS", bv)], w=[("XT", h_)])
                P.op("dve", (lambda e, tmp=tmp, ti=ti, c0=c0: e.tensor_tensor(out=tmp, in0=tmp, in1=gate_vec(ti, c0, 512), op=ALU.mult)),
                     r=["MV", "GSV"], w=[("XT", h_)])
                xt = xtile(ti)
                P.op("pool", (lambda e, tmp=tmp, xt=xt, c0=c0: e.tensor_tensor(out=xt[:, c0:c0 + 512], in0=xt[:, c0:c0 + 512], in1=tmp,
                                                                               op=ALU.add)), r=[("XT", h_)], w=["X"])
        P.barrier()
        if l == 0 and own:
            for j in range(2):
                dbg_dump("xmix0_%d" % j, X[:, j, :], [128, D], F32, keys=["X"])
            dbg_dump("xsmix0", XS[0:4, :], [4, D], F32, keys=["X"])
        mlp(l, tiles, rows, own)

    def mlp(l, tiles, rows, own):
        base = l * 6 * D
        rms_modulate(rows, norm_w[l, 1:2, :], base + 4 * D, base + 3 * D)
        for ti in tiles:
            to_feature_major(ti, HT[0:ntok(ti), ti, :])
        P.barrier()
        load_gate(l, 5, own)
        tblocks = [(0, 512), (512, 512)] + ([(TT, 4)] if own else [])
        for hc in range(DFF // 512):
            sU = load_w(mlp_up[l][:, hc * 512:(hc + 1) * 512], NCT, 512)
            for ht in range(4):
                for bi, (t0, nt) in enumerate(tblocks):
                    bank = 4 + (ht * 3 + bi) % 4
                    for k in range(NCT):
                        P.op("pe", (lambda e, k=k, sU=sU, bank=bank, ht=ht, t0=t0, nt=nt:
                                    e.matmul(PS[bank][:, 0:nt], lhsT=WB[:, sU, k, ht * 128:(ht + 1) * 128], rhs=HF[:, k, t0:t0 + nt],
                                             start=(k == 0), stop=(k == NCT - 1))), r=[("WB", sU), "HF"], w=[("PS", bank)])
                    P.op("act", (lambda e, bank=bank, ht=ht, t0=t0, nt=nt: e.activation(out=HID[:, ht, t0:t0 + nt], in_=PS[bank][:, 0:nt],
                                                                                        func=AF.Relu)), r=[("PS", bank)], w=[("HID", ht)])
                    P.op("pool", (lambda e, ht=ht, t0=t0, nt=nt: e.tensor_tensor(out=HID[:, ht, t0:t0 + nt], in0=HID[:, ht, t0:t0 + nt],
                                                                                in1=HID[:, ht, t0:t0 + nt], op=ALU.mult)),
                         r=[], w=[("HID", ht)])
            s = wslot[0]; wslot[0] ^= 1
            WD = WB[:, s].rearrange("p k n -> p (k n)").rearrange("p (a n) -> p a n", a=4)
            P.dma("pool", WD, mlp_down[l][hc * 512:(hc + 1) * 512, :].rearrange("(a p) n -> p a n", p=128), w=[("WB", s)])
            for ti in tiles:
                n = ntok(ti)
                for nb in range(4):
                    bank = (ti * 4 + nb) % 4
                    for kt in range(4):
                        P.op("pe", (lambda e, kt=kt, bank=bank, ti=ti, nb=nb, n=n, WD=WD:
                                    e.matmul(PS[bank][0:n, 0:512], lhsT=hid_cols(ti)[:, kt], rhs=WD[:, kt, nb * 512:(nb + 1) * 512],
                                             start=(kt == 0), stop=(kt == 3))), r=[("WB", s)] + [("HID", q) for q in range(4)],
                             w=[("PS", bank)])
                    resid_add(ti, nb * 512, 512, bank)
        P.barrier()
    KWT = sb("KWT", [128, 128], F32)
    SLW = sb("SLW", [128, 256], F32)
    LAMT = sb("LAMT", [128, 8], F32)
    ktsD = dscr("ktsD", [16, 128, 4], BF16)
    vsD = dscr("vsD", [4, D], BF16)
    JNK = USAMP

    def qk_norm_evac(ti, c0, bank, wt, out_dram):
        n = ntok(ti)
        h_ = xh[0]; xh[0] ^= 1
        tmp = XTMP[0:n, h_ * 512:(h_ + 1) * 512]
        sm = SM[0:n, 16 + 4 * h_:20 + 4 * h_]
        P.op("act", (lambda e, tmp=tmp, n=n: e.copy(out=tmp, in_=PS[bank][0:n, 0:512])), r=[("PS", bank)], w=[("XT", h_)])
        P.op("dve", (lambda e, sm=sm: e.memset(sm, 0.0)), r=[], w=[("SMQ", h_)])
        for q in range(4):
            P.op("act", (lambda e, q=q, n=n, tmp=tmp, h_=h_: e.activation(out=JNK[0:n, :], in_=tmp[:, q * 128:(q + 1) * 128], func=AF.Square,
                                                                       accum_out=SM[0:n, 16 + 4 * h_ + q:17 + 4 * h_ + q])),
                 r=[("XT", h_)], w=["JNK", ("SMQ", h_)])
        P.op("dve", (lambda e, sm=sm: e.tensor_scalar(out=sm, in0=sm, scalar1=1.0 / 128, scalar2=EPS, op0=ALU.mult, op1=ALU.add)),
             r=[], w=[("SMQ", h_)])
        P.op("act", (lambda e, sm=sm: e.activation(out=sm, in_=sm, func=AF.Sqrt)), r=[], w=[("SMQ", h_)])
        P.op("dve", (lambda e, sm=sm: e.reciprocal(out=sm, in_=sm)), r=[], w=[("SMQ", h_)])
        t3 = tmp.rearrange("p (q d) -> p q d", d=128)
        P.op("dve", (lambda e, t3=t3, sm=sm, n=n: e.tensor_tensor(out=t3, in0=t3, in1=sm.unsqueeze(2).broadcast_to([n, 4, 128]), op=ALU.mult)),
             r=[("SMQ", h_)], w=[("XT", h_)])
        P.op("dve", (lambda e, t3=t3, n=n: e.tensor_tensor(out=t3, in0=t3, in1=wt[0:n, :].unsqueeze(1).broadcast_to([n, 4, 128]), op=ALU.mult)),
             r=["KWT"], w=[("XT", h_)])
        if out_dram is not None:
            P.dma("sp", out_dram, tmp, r=[("XT", h_)])
        P.op("pool", (lambda e, tmp=tmp, n=n, ti=ti, c0=c0: e.tensor_copy(out=HT[0:n, ti, c0:c0 + 512], in_=tmp)), r=[("XT", h_)], w=["HT"])

    vst = [0]

    def kv_phase(own):
        tiles = list(range(8)) + ([8] if own else [])
        rows = [(list(range(8)), 0)] + ([([8], 1)] if own else [])
        base = 24 * D
        rms_modulate(rows, kv_norm_w[0:1, :], base + D, base)
        for ti in tiles:
            to_feature_major(ti, HT[0:ntok(ti), ti, :])
        bcast_row(KWT[:, :], k_norm_w[0:1, :], key="KWT")
        P.barrier()
        tokbase = TT if own else 0
        vDv = vD[tokbase:tokbase + TT, :].rearrange("(k j) d -> k j d", j=8)
        kov = k_o.rearrange("(k j) d -> k j d", j=8); vov = v_o.rearrange("(k j) d -> k j d", j=8)

        def evac(ti, c0, blk, bank):
            n = ntok(ti)
            if c0 < D:
                od = None
                if own:
                    od = kov[:, ti, c0:c0 + 512] if ti < 8 else k_s[:, c0:c0 + 512]
                qk_norm_evac(ti, c0, bank, KWT, od)
            else:
                c = c0 - D
                h_ = xh[0]; xh[0] ^= 1
                tmp = XTMP[0:n, h_ * 512:(h_ + 1) * 512]
                P.op("act", (lambda e, tmp=tmp, n=n: e.copy(out=tmp, in_=PS[bank][0:n, 0:512])), r=[("PS", bank)], w=[("XT", h_)])
                if own:
                    P.dma("sp", vov[:, ti, c:c + 512] if ti < 8 else v_s[:, c:c + 512], tmp, r=[("XT", h_)])
                vs_ = vst[0]; vst[0] = (vst[0] + 1) % 4
                stg = HID[0:n, vs_, 0:512]
                P.op("pool", (lambda e, stg=stg, tmp=tmp: e.tensor_copy(out=stg, in_=tmp)), r=[("XT", h_)], w=[("VST", vs_)])
                P.dma("sp", vDv[:, ti, c:c + 512] if ti < 8 else vsD[:, c:c + 512], stg, r=[("VST", vs_)], w=["vD"])

        linear_tok(kv_w, 2 * D, tiles, evac)
        P.barrier()
        for ti in tiles:
            to_feature_major(ti, HT[0:ntok(ti), ti, :])
        P.dma("sp", ktD[:, :, tokbase:tokbase + TT].rearrange("h d t -> d h t"), HF[:, :, 0:TT], r=["HF"], w=["ktD"])
        if own:
            P.dma("sp", ktsD.rearrange("h d t -> d h t"), HF[:, :, TT:TT + 4], r=["HF"], w=["ktD"])
        P.barrier()

    WBraw = WB[:].rearrange("p s k n -> p (s k n)")
    KTh = WBraw[:, 0:4096].rearrange("p (m t) -> p m t", m=2)
    VHa = WBraw[:, 4096:4096 + 16 * 258].rearrange("p (t v) -> p t v", v=258)
    ETL = WBraw[:, 8224:8224 + 4 * 128].rearrange("p (s q) -> p s q", s=4)
    KPf = WBraw[:, 0:4096].bitcast(F32)
    VPf = WBraw[:, 4096:8192].bitcast(F32)
    KPb = WBraw[:, 8192:10240]
    VPb = WBraw[:, 10240:12288]
    KTP = WBraw[:, 12288:14336].rearrange("p (c k) -> p c k", k=128)
    ESb = WBraw[:, 14336:14400]
    ESUM = WBraw[:, 14400:14528].bitcast(F32)
    ES4 = WBraw[:, 14528:14592]
    KTS = WBraw[:, 14592:14656].rearrange("p (c k) -> p c k", k=4)
    HIDf = HID[:].rearrange("p a b -> p (a b)")
    VSs = HIDf[:, 2048:4096]
    PTI = WBraw[:, 14656:14912].bitcast(I32)
    JNK2 = WBraw[:, 16000:16256]
    PTF = WBraw[:, 14912:15168].bitcast(F32)

    def build_pti():
        P.dma("sp", PTI, ptab[0:1, :].partition_broadcast(128)[:, 0, :], w=["PTI"])
        P.op("dve", lambda e: e.tensor_copy(out=PTF, in_=PTI), r=[], w=["PTI"])
        P.op("dve", lambda e: e.tensor_scalar(out=PTF, in0=PTF, scalar1=128.0, scalar2=FLG[:, 2:3], op0=ALU.mult, op1=ALU.add),
             r=["FLG"], w=["PTI"])
        P.op("dve", lambda e: e.tensor_copy(out=PTI, in_=PTF), r=[], w=["PTI"])

    def compute_lam(jb, l):
        lam_init = 0.8 - 0.6 * float(np.exp(-0.3 * l))
        for idx, (a_, b_) in enumerate(((lq1, lk1), (lq2, lk2))):
            bcast_row(XTMP[:, 0:128], a_[jb:jb + 1, :], key="XTMP")
            bcast_row(XTMP[:, 128:256], b_[jb:jb + 1, :], key="XTMP")
            P.op("dve", lambda e: e.tensor_tensor(out=XTMP[:, 256:384], in0=XTMP[:, 0:128], in1=XTMP[:, 128:256], op=ALU.mult),
                 r=["XTMP"], w=["XTMP"])
            P.op("dve", (lambda e, idx=idx: e.tensor_reduce(out=LAMT[:, 2 + idx:3 + idx], in_=XTMP[:, 256:384], axis=AX.X, op=ALU.add)),
                 r=["XTMP"], w=["LAMT"])
            P.op("act", (lambda e, idx=idx: e.activation(out=LAMT[:, 2 + idx:3 + idx], in_=LAMT[:, 2 + idx:3 + idx], func=AF.Exp)),
                 r=[], w=["LAMT"])
        P.op("dve", lambda e: e.tensor_tensor(out=LAMT[:, 0:1], in0=LAMT[:, 2:3], in1=LAMT[:, 3:4], op=ALU.subtract), r=[], w=["LAMT"])
        P.op("dve", lambda e: e.tensor_scalar(out=LAMT[:, 0:1], in0=LAMT[:, 0:1], scalar1=lam_init, scalar2=None, op0=ALU.add), r=[], w=["LAMT"])
        P.op("dve", lambda e: e.tensor_scalar(out=LAMT[:, 1:2], in0=LAMT[:, 0:1], scalar1=-1.0, scalar2=None, op0=ALU.mult), r=[], w=["LAMT"])
        return lam_init

    def subln_store(src, n, dst, lam_init, smcol):
        sm = SM[0:n, smcol:smcol + 1]
        P.op("dve", (lambda e, sm=sm: e.memset(sm, 0.0)), r=[], w=["SMS"])
        P.op("act", (lambda e, sm=sm, n=n, src=src: e.activation(out=JNK2[0:n, :], in_=src,
                                                               func=AF.Square, accum_out=sm)), r=["OTMP"], w=["SMS", "JNK2"])
        P.op("dve", (lambda e, sm=sm: e.tensor_scalar(out=sm, in0=sm, scalar1=1.0 / 256, scalar2=EPS, op0=ALU.mult, op1=ALU.add)),
             r=[], w=["SMS"])
        P.op("act", (lambda e, sm=sm: e.activation(out=sm, in_=sm, func=AF.Sqrt)), r=[], w=["SMS"])
        P.op("dve", (lambda e, sm=sm: e.reciprocal(out=sm, in_=sm)), r=[], w=["SMS"])
        P.op("dve", (lambda e, sm=sm, src=src: e.tensor_scalar(out=src, in0=src, scalar1=sm, scalar2=1.0 - lam_init, op0=ALU.mult, op1=ALU.mult)),
             r=["SMS"], w=["OTMP"])
        P.op("dve", (lambda e, src=src, n=n, dst=dst: e.tensor_tensor(out=dst, in0=src, in1=SLW[0:n, :], op=ALU.mult)),
             r=["OTMP", "SLW"], w=["HT"])

    def sample_attention(lam_init):
        build_pti()
        P.dma("sp", KTS, ktsD.rearrange("h d t -> d h t"), w=["KTS"])
        P.dma("sp", VSs[0:4, :], vsD[:, :], w=["VSs"])
        P.op("dve", lambda e: e.memset(ESUM, 0.0), w=["ESUM"])
        QTs = HF[:, :, TT:TT + 4]
        for pg in range(NPAGES + 1):
            last = pg == NPAGES
            if not last:
                P.dmafn("pool", (lambda e, pg=pg: e.indirect_dma_start(
                    out=KPf, out_offset=None, in_=cache_k[:, :],
                    in_offset=bass.IndirectOffsetOnAxis(ap=PTI[:, pg:pg + 1], axis=0))), r=["PTI"], w=["KPf"])
                P.dmafn("pool", (lambda e, pg=pg: e.indirect_dma_start(
                    out=VPf, out_offset=None, in_=cache_v[:, :],
                    in_offset=bass.IndirectOffsetOnAxis(ap=PTI[:, pg:pg + 1], axis=0))), r=["PTI"], w=["VPf"])
                P.op("act", lambda e: e.copy(out=KPb, in_=KPf), r=["KPf"], w=["KPb"])
                P.op("dve", lambda e: e.tensor_copy(out=VPb, in_=VPf), r=["VPf"], w=["VPb"])
                for half in range(2):
                    bank = 4 + half
                    for c in range(8):
                        hm = half * 8 + c
                        P.op("pe", (lambda e, hm=hm, c=c, bank=bank: e.transpose(out=psbf(bank)[:, c * 128:(c + 1) * 128],
                                                                                in_=KPb[:, hm * 128:(hm + 1) * 128], identity=IDB[:])),
                             r=["KPb", "IDB"], w=[("PS", bank)])
                    P.op("dve" if half else "act",
                         (lambda e, half=half, bank=bank: (e.tensor_copy if half else e.copy)(
                             out=KTP[:, half * 8:(half + 1) * 8, :], in_=psbf(bank)[:, 0:1024].rearrange("p (c k) -> p c k", k=128))),
                         r=[("PS", bank)], w=["KTP"])
                for hm in range(16):
                    P.op("pe", (lambda e, hm=hm: e.matmul(PS[6][:, hm * 4:(hm + 1) * 4], lhsT=KTP[:, hm, :], rhs=QTs[:, hm, :],
                                                          start=True, stop=True)), r=["KTP", "HF"], w=[("PS", 6)])
                P.op("act", lambda e: e.activation(out=ESb, in_=PS[6][:, 0:64], func=AF.Exp, scale=SCALE), r=[("PS", 6)], w=["ESb"])
                P.op("pool", lambda e: e.tensor_tensor(out=ESUM, in0=ESUM, in1=ESb, op=ALU.add), r=["ESb"], w=["ESUM"])
                for h in range(NH):
                    P.op("pe", (lambda e, h=h, pg=pg: e.matmul(PS[h // 2][0:8, (h % 2) * 256:(h % 2 + 1) * 256], lhsT=ESb[:, h * 8:(h + 1) * 8],
                                                               rhs=VPb[:, h * 256:(h + 1) * 256], start=(pg == 0), stop=False)),
                         r=["ESb", "VPb"], w=[("PS", h // 2)])
            else:
                for hm in range(16):
                    P.op("pe", (lambda e, hm=hm: e.matmul(PS[6][0:4, hm * 4:(hm + 1) * 4], lhsT=KTS[:, hm, :], rhs=QTs[:, hm, :],
                                                          start=True, stop=True)), r=["KTS", "HF"], w=[("PS", 6)])
                P.op("act", lambda e: e.activation(out=ES4[0:4, :], in_=PS[6][0:4, 0:64], func=AF.Exp, scale=SCALE), r=[("PS", 6)], w=["ES4"])
                P.op("dve", lambda e: e.tensor_tensor(out=ES4[0:4, :].rearrange("p (c q) -> p c q", q=4),
                                                      in0=ES4[0:4, :].rearrange("p (c q) -> p c q", q=4),
                                                      in1=CAUS[0:4, 0:4].unsqueeze(1).broadcast_to([4, 16, 4]), op=ALU.mult),
                     r=["CAUS"], w=["ES4"])
                P.op("pool", lambda e: e.tensor_tensor(out=ESUM[0:4, :], in0=ESUM[0:4, :], in1=ES4[0:4, :], op=ALU.add), r=["ES4"], w=["ESUM"])
                for h in range(NH):
                    P.op("pe", (lambda e, h=h: e.matmul(PS[h // 2][0:8, (h % 2) * 256:(h % 2 + 1) * 256], lhsT=ES4[0:4, h * 8:(h + 1) * 8],
                                                        rhs=VSs[0:4, h * 256:(h + 1) * 256], start=False, stop=True)),
                         r=["ES4", "VSs"], w=[("PS", h // 2)])
        P.op("dve", lambda e: e.tensor_copy(out=ESb, in_=ESUM), r=["ESUM"], w=["ESb"])
        for h in range(NH):
            P.op("pe", (lambda e, h=h: e.matmul(PS[7][0:8, h:h + 1], lhsT=ESUM[:, h * 8:(h + 1) * 8], rhs=ONESF[:, 0:1],
                                                start=True, stop=True)), r=["ESUM", "ONESF"], w=[("PS", 7)])
        DEN = SM[0:8, 32:40]
        P.op("dve", lambda e: e.reciprocal(out=DEN, in_=PS[7][0:8, 0:8]), r=[("PS", 7)], w=["DEN"])
        OT = XTMP[0:8, :]
        for hq in range(2):
            for hh in range(4):
                h = hq * 4 + hh
                P.op("dve", (lambda e, h=h, hh=hh: e.tensor_scalar(out=OT[:, hh * 256:(hh + 1) * 256], in0=PS[h // 2][0:8, (h % 2) * 256:(h % 2 + 1) * 256],
                                                                   scalar1=DEN[:, h:h + 1], scalar2=None, op0=ALU.mult)),
                     r=[("PS", h // 2), "DEN"], w=["XTMP"])
            P.dma("sp", OM1[0:4, :], OT[4:8, :], r=["XTMP"], w=["OM1"])
            for hh in range(4):
                h = hq * 4 + hh
                src = OT[0:4, hh * 256:(hh + 1) * 256]
                P.op("dve", (lambda e, src=src, hh=hh: e.scalar_tensor_tensor(out=src, in0=OM1[0:4, hh * 256:(hh + 1) * 256], scalar=LAMT[0:4, 1:2],
                                                                             in1=src, op0=ALU.mult, op1=ALU.add)),
                     r=["OM1", "LAMT", "XTMP"], w=["XTMP", "OTMP"])
                subln_store(src, 4, HT[0:4, 8, h * 256:(h + 1) * 256], lam_init, 40 + h)
            P.barrier()
        P.barrier()

    ONESF = sb("ONESF", [128, 2], F32)
    P.op("pool", lambda e: e.memset(ONESF[:, :], 1.0), w=["ONESF"])
    OM1 = HIDf[0:4, 0:2048].bitcast(F32)

    def prompt_attention(lam_init):
        P.op("pool", lambda e: e.memset(VHa[:, :, 256:258], 1.0), w=["VHa"])
        vDn = vD.rearrange("(t p) d -> p t d", p=128)
        for h in range(NH):
            P.dma("sp", KTh, ktD[2 * h:2 * h + 2].rearrange("m d t -> d m t"), r=["ktD"], w=["KTh"])
            P.dma("sp", VHa[:, :, 0:256], vDn[:, :, h * 256:(h + 1) * 256], r=["vD"], w=["VHa"])
            for qi in range(8):
                nkt = 8 + qi + 1
                ob = 2 + 2 * (qi % 2)
                for m in range(2):
                    for kt in range(nkt):
                        sbk = kt % 2
                        es = (m * 17 + kt) % 4
                        P.op("pe", (lambda e, m=m, kt=kt, sbk=sbk, h=h, qi=qi: e.matmul(
                            PS[sbk][:, 0:128], lhsT=KTh[:, m, kt * 128:(kt + 1) * 128], rhs=HF[:, 2 * h + m, qi * 128:(qi + 1) * 128],
                            start=True, stop=True)), r=["KTh", "HF"], w=[("PS", sbk)])
                        if kt < 8:
                            P.op("act", (lambda e, sbk=sbk, es=es: e.activation(out=ETL[:, es, :], in_=PS[sbk][:, 0:128], func=AF.Exp,
                                                                              scale=SCALE, bias=FLG[:, 1:2])), r=[("PS", sbk), "FLG"], w=[("ET", es)])
                        else:
                            P.op("act", (lambda e, sbk=sbk, es=es: e.activation(out=ETL[:, es, :], in_=PS[sbk][:, 0:128], func=AF.Exp,
                                                                              scale=SCALE)), r=[("PS", sbk)], w=[("ET", es)])
                        if kt == nkt - 1:
                            P.op("pool", (lambda e, es=es: e.tensor_tensor(out=ETL[:, es, :], in0=ETL[:, es, :], in1=CAUS[:, :], op=ALU.mult)),
                                 r=["CAUS"], w=[("ET", es)])
                        P.op("pe", (lambda e, m=m, kt=kt, es=es, ob=ob, nkt=nkt: e.matmul(
                            PS[ob + m][:, 0:257], lhsT=ETL[:, es, :], rhs=VHa[:, kt, 0:257], start=(kt == 0), stop=(kt == nkt - 1))),
                            r=[("ET", es), "VHa"], w=[("PS", ob + m)])
                h_ = xh[0]; xh[0] ^= 1
                src = XTMP[:, h_ * 512:h_ * 512 + 256]
                rr = SM[:, 48 + 2 * h_:50 + 2 * h_]
                P.op("dve", (lambda e, rr=rr, ob=ob: e.reciprocal(out=rr[:, 0:1], in_=PS[ob][:, 256:257])), r=[("PS", ob)], w=[("RR", h_)])
                P.op("dve", (lambda e, rr=rr, ob=ob: e.reciprocal(out=rr[:, 1:2], in_=PS[ob + 1][:, 256:257])), r=[("PS", ob + 1)], w=[("RR", h_)])
                P.op("dve", (lambda e, rr=rr: e.tensor_tensor(out=rr[:, 1:2], in0=rr[:, 1:2], in1=LAMT[:, 1:2], op=ALU.mult)), r=["LAMT"], w=[("RR", h_)])
                P.op("dve", (lambda e, rr=rr, ob=ob, src=src: e.tensor_scalar(out=src, in0=PS[ob][:, 0:256], scalar1=rr[:, 0:1], scalar2=None,
                                                                              op0=ALU.mult)), r=[("PS", ob), ("RR", h_)], w=["OTMP", ("XT", h_)])
                P.op("dve", (lambda e, rr=rr, ob=ob, src=src: e.scalar_tensor_tensor(out=src, in0=PS[ob + 1][:, 0:256], scalar=rr[:, 1:2], in1=src,
                                                                                     op0=ALU.mult, op1=ALU.add)),
                     r=[("PS", ob + 1), ("RR", h_)], w=["OTMP", ("XT", h_)])
                subln_store(src, 128, HT[:, qi, h * 256:(h + 1) * 256], lam_init, 56 + h_)
        P.barrier()

    def b_layer(l):
        jb = l - NA
        tiles = list(range(9))
        rows = [(list(range(8)), 0), ([8], 1)]
        base = l * 6 * D
        rms_modulate(rows, norm_w[l, 0:1, :], base + D, base)
        for ti in tiles:
            to_feature_major(ti, HT[0:ntok(ti), ti, :])
        bcast_row(KWT[:, :], q_norm_w[jb:jb + 1, :], key="KWT")
        bcast_row(SLW[:, :], subln_w[jb:jb + 1, :], key="SLW")
        lam_init = compute_lam(jb, l)
        P.barrier()
        linear_tok(attn_wq[jb], D, tiles, lambda ti, c0, blk, bank: qk_norm_evac(ti, c0, bank, KWT, None))
        P.barrier()
        for ti in tiles:
            to_feature_major(ti, HT[0:ntok(ti), ti, :])
        P.barrier()
        sample_attention(lam_init)
        prompt_attention(lam_init)
        for qi in range(8):
            to_feature_major(qi, HT[:, qi, :], dstF=HF[:, :, qi * 128:(qi + 1) * 128])
        to_feature_major(8, HT[0:4, 8, :])
        P.barrier()
        load_gate(l, 2, True)
        linear_tok(attn_wo[jb], D, tiles, lambda ti, c0, blk, bank: resid_add(ti, c0, blk, bank))
        P.barrier()
        mlp(l, tiles, rows, True)

    load_x(xp, False)
    a_layer(0, False)
    a_layer(1, False)
    kv_phase(False)
    load_x(xo, True)
    a_layer(0, True)
    a_layer(1, True)
    kv_phase(True)
    b_layer(2)
    b_layer(3)
    ov = y_o.rearrange("(k j) d -> k j d", j=8)
    for j in range(8):
        P.dma("sp", ov[:, j, :], X[:, j, :], r=["X"])
    P.dma("sp", y_s[:, :], XS[0:4, :], r=["X"])
    P.emit(nc, stack)
    return nc, stack


_CACHE = {}


def kernel(x_prompt, x_sample, state_ssm_re, state_ssm_im, cache_k, cache_v, page_table, c_prompt, c_sample,
           ada_w, ada_b, norm_w, mlp_up, mlp_down, ssm_lambda_re, ssm_lambda_im, ssm_log_step,
           ssm_b_re, ssm_b_im, ssm_c_re, ssm_c_im, ssm_d, glu_w, kv_ada_w, kv_ada_b, kv_norm_w, kv_w,
           k_norm_w, attn_wq, q_norm_w, lambda_q1, lambda_k1, lambda_q2, lambda_k2, subln_w, attn_wo):
    f32 = np.float32
    A = lambda a: np.ascontiguousarray(np.asarray(a))
    if "nc" not in _CACHE:
        _CACHE["nc"] = build_program()
    nc, _stack = _CACHE["nc"]
    x_prompt = A(x_prompt); x_sample = A(x_sample)
    ck = A(cache_k).reshape(1280 * 128, D); cv = A(cache_v).reshape(1280 * 128, D)
    ident = np.eye(128, dtype=f32)
    idx = np.arange(128)
    tri = (idx[:, None] // 16 <= idx[None, :] // 16).astype(f32)
    caus = (idx[:, None] <= idx[None, :]).astype(f32)
    shared = {
        "cache_k": ck, "cache_v": cv,
        "ada_w": A(ada_w), "ada_b": A(ada_b), "norm_w": A(norm_w), "mlp_up": A(mlp_up), "mlp_down": A(mlp_down),
        "ssm_lambda_re": A(ssm_lambda_re), "ssm_lambda_im": A(ssm_lambda_im), "ssm_log_step": A(ssm_log_step),
        "ssm_b_re": A(ssm_b_re), "ssm_b_im": A(ssm_b_im), "ssm_c_re": A(ssm_c_re), "ssm_c_im": A(ssm_c_im),
        "ssm_d": A(ssm_d), "glu_w": A(glu_w), "kv_ada_w": A(kv_ada_w), "kv_ada_b": A(kv_ada_b).reshape(1, -1),
        "kv_norm_w": A(kv_norm_w).reshape(1, -1), "kv_w": A(kv_w), "k_norm_w": A(k_norm_w).reshape(1, -1),
        "attn_wq": A(attn_wq), "q_norm_w": A(q_norm_w), "lambda_q1": A(lambda_q1), "lambda_k1": A(lambda_k1),
        "lambda_q2": A(lambda_q2), "lambda_k2": A(lambda_k2), "subln_w": A(subln_w), "attn_wo": A(attn_wo),
        "cident": ident, "ctri": tri, "ccaus": caus,
    }
    zeros_x = np.zeros((TT, D), f32)
    in_maps = []
    for i in range(8):
        b, half = i // 2, i % 2
        fl = np.zeros((128, 3), f32)
        fl[:, 0] = float(half); fl[:, 1] = (float(half) - 1.0) * 30000.0; fl[:, 2] = np.arange(128)
        m = dict(shared)
        m.update({
            "xo": A(x_prompt[b, half * TT:(half + 1) * TT]),
            "xp": A(x_prompt[b, 0:TT]) if half == 1 else zeros_x,
            "xs": A(x_sample[i]),
            "cvec": A(np.stack([np.asarray(c_prompt)[b], np.asarray(c_sample)[i]]).astype(f32)),
            "s0re": A(np.asarray(state_ssm_re)[:, i]), "s0im": A(np.asarray(state_ssm_im)[:, i]),
            "ptab": A(np.asarray(page_table)[i:i + 1].astype(np.int32)),
            "flag": fl,
        })
        in_maps.append(m)
    res = run_bass_kernel_spmd(nc, in_maps, core_ids=list(range(8)))
    R = res.results
    _CACHE["res"] = R
    B, T = x_prompt.shape[0], x_prompt.shape[1]
    y_prompt = np.zeros((B, T, D), f32); k_prompt = np.zeros((B, T, NH, 256), f32); v_prompt = np.zeros((B, T, NH, 256), f32)
    spr = np.zeros((NA, B, G, NST), f32); spi = np.zeros((NA, B, G, NST), f32)
    y_sample = np.zeros((8, 4, D), f32); k_sample = np.zeros((8, 4, NH, 256), f32); v_sample = np.zeros((8, 4, NH, 256), f32)
    ssr = np.zeros((NA, 8, G, NST), f32); ssi = np.zeros((NA, 8, G, NST), f32)
    for i in range(8):
        b, half = i // 2, i % 2
        sl = slice(half * TT, (half + 1) * TT)
        y_prompt[b, sl] = R[i]["y_o"]
        k_prompt[b, sl] = R[i]["k_o"].reshape(TT, NH, 256)
        v_prompt[b, sl] = R[i]["v_o"].reshape(TT, NH, 256)
        if half == 1:
            spr[:, b] = R[i]["sp_re"]; spi[:, b] = R[i]["sp_im"]
        y_sample[i] = R[i]["y_s"]
        k_sample[i] = R[i]["k_s"].reshape(4, NH, 256); v_sample[i] = R[i]["v_s"].reshape(4, NH, 256)
        ssr[:, i] = R[i]["ss_re"]; ssi[:, i] = R[i]["ss_im"]
    return (y_prompt, y_sample, spr, spi, k_prompt, v_prompt, ssr, ssi, k_sample, v_sample)
```

```python
import numpy as np
import ml_dtypes
import concourse.bass as bass
import concourse.mybir as mybir
from concourse.bass_utils import run_bass_kernel_spmd

F32 = mybir.dt.float32
BF16 = mybir.dt.bfloat16
I32 = mybir.dt.int32
AF = mybir.ActivationFunctionType
ALU = mybir.AluOpType
AX = mybir.AxisListType

D = 2048
NCT = 16
TT = 1024
G = 128
NST = 64
DEPTH = 4
NA = 2
DFF = 8192
NH = 8
EPS = 1e-6
SCALE = 128 ** -0.5
NPAGES = 128
GB = 16
TWO_PI = 2.0 * np.pi
DEBUG = False


class Prog:
    CE = ("pe", "act", "dve", "pool")
    NDS = 12

    def __init__(self):
        self.ins = []
        self.lastw = {}
        self.readers = {}
        self.dma_count = {"sp": 0, "pool": 0, "act": 0}
        self.floor = []
        self.deferred = None

    def _add(self, eng, fn, r, w, kind, q=None):
        calls = []

        class _Rec:
            def __getattr__(self_, name):
                def f(*a, **kw):
                    calls.append((name, a, kw))
                    return None
                return f
        fn(_Rec())
        assert len(calls) == 1, calls
        _name, _a, _kw = calls[0]
        fn = (lambda e, _name=_name, _a=_a, _kw=_kw: getattr(e, _name)(*_a, **_kw))
        if self.deferred is not None:
            self.deferred.append((eng, fn, list(r), list(w), kind, q))
            return None
        return self._add2(eng, fn, r, w, kind, q)

    def flush(self, n=None):
        d = self.deferred
        self.deferred = None
        k = len(d) if n is None else min(n, len(d))
        for it in d[:k]:
            self._add2(*it)
        rest = d[k:]
        self.deferred = rest if (n is not None) else None
        return len(rest)

    def _add2(self, eng, fn, r, w, kind, q=None):
        deps = set(self.floor)
        for k in list(r) + list(w):
            if k in self.lastw:
                deps.add(self.lastw[k])
        for k in w:
            deps.update(self.readers.get(k, ()))
        idx = len(self.ins)
        rec = dict(eng=eng, fn=fn, deps=deps, kind=kind, q=q)
        if kind == "dma":
            n = self.dma_count[q]
            self.dma_count[q] = n + 1
            rec["slot"] = n % self.NDS
            rec["gen"] = n // self.NDS
        self.ins.append(rec)
        for k in r:
            self.readers.setdefault(k, []).append(idx)
        for k in w:
            self.lastw[k] = idx
            self.readers[k] = []
        return idx

    def op(self, eng, fn, r=(), w=()):
        return self._add(eng, fn, r, w, "op")

    def dma(self, q, out, in_, r=(), w=(), **kw):
        eng = {"sp": "sp", "pool": "pool", "act": "act"}[q]
        return self._add(eng, lambda e: e.dma_start(out=out, in_=in_, **kw), r, w, "dma", q=q)

    def dmafn(self, q, fn, r=(), w=()):
        return self._add(q, fn, r, w, "dma", q=q)

    def barrier(self):
        assert self.deferred is None
        last = {}
        for i, rec in enumerate(self.ins):
            if rec["kind"] == "dma":
                last[("dma", rec["q"], rec["slot"])] = i
            else:
                last[rec["eng"]] = i
        self.floor = list(last.values())

    def emit(self, nc, stack):
        ins = self.ins
        sems = {e: stack.enter_context(nc.semaphore("c_" + e)) for e in self.CE}
        dsems = {(q, s): stack.enter_context(nc.semaphore("d_%s_%d" % (q, s)))
                 for q in ("sp", "pool", "act") for s in range(self.NDS)}
        needed = set()
        for rec in ins:
            for d in rec["deps"]:
                needed.add(d)
        cnt = {e: 0 for e in self.CE}
        for i, rec in enumerate(ins):
            if rec["kind"] == "op":
                if i in needed:
                    cnt[rec["eng"]] += 1
                    rec["sig"] = cnt[rec["eng"]]
                else:
                    rec["sig"] = None
        per_eng = {e: [] for e in ("pe", "act", "dve", "pool", "sp")}
        for i, rec in enumerate(ins):
            per_eng[rec["eng"]].append(i)
        prev_on_slot = {}
        for i, rec in enumerate(ins):
            if rec["kind"] == "dma":
                key = (rec["q"], rec["slot"])
                rec["prev"] = prev_on_slot.get(key)
                prev_on_slot[key] = i
        block = stack.enter_context(nc.Block())

        def run(engname, e):
            waited_c = {x: 0 for x in self.CE}
            waited_d = {}
            for i in per_eng[engname]:
                rec = ins[i]
                deps = set(rec["deps"])
                if rec["kind"] == "dma" and rec["prev"] is not None:
                    deps.add(rec["prev"])
                cmax = {}
                for d in deps:
                    dr = ins[d]
                    if dr["kind"] == "dma":
                        key = (dr["q"], dr["slot"])
                        val = 16 * (dr["gen"] + 1)
                        if waited_d.get(key, 0) < val:
                            e.wait_ge(dsems[key], val)
                            waited_d[key] = val
                    else:
                        if dr["eng"] == "pe" and engname == "pe":
                            continue
                        cmax[dr["eng"]] = max(cmax.get(dr["eng"], 0), dr["sig"])
                for de, v in cmax.items():
                    if waited_c[de] < v:
                        e.wait_ge(sems[de], v)
                        waited_c[de] = v
                inst = rec["fn"](e)
                if rec["kind"] == "dma":
                    inst.then_inc(dsems[(rec["q"], rec["slot"])], 16)
                elif rec["sig"] is not None:
                    inst.then_inc(sems[rec["eng"]], 1)
            if engname in ("sp", "pool", "act"):
                for (q, s), i in prev_on_slot.items():
                    if q == engname:
                        val = 16 * (ins[i]["gen"] + 1)
                        if waited_d.get((q, s), 0) < val:
                            e.wait_ge(dsems[(q, s)], val)

        block.tensor(lambda e: run("pe", e))
        block.scalar(lambda e: run("act", e))
        block.vector(lambda e: run("dve", e))
        block.gpsimd(lambda e: run("pool", e))
        block.sync(lambda e: run("sp", e))


def build_program():
    from contextlib import ExitStack
    nc = bass.Bass("TRN2", target_bir_lowering=False)
    P = Prog()
    stack = ExitStack()

    def din(name, shape, dt=F32):
        return nc.dram_tensor(name, list(shape), dt, kind="ExternalInput").ap()

    def dout(name, shape, dt=F32):
        return nc.dram_tensor(name, list(shape), dt, kind="ExternalOutput").ap()

    def dscr(name, shape, dt=F32):
        if DEBUG:
            return nc.dram_tensor(name, list(shape), dt, kind="ExternalOutput").ap()
        return nc.dram_tensor(name, list(shape), dt).ap()

    xo = din("xo", [TT, D]); xp = din("xp", [TT, D]); xs = din("xs", [4, D])
    cvec = din("cvec", [2, D])
    s0re = din("s0re", [NA, G, NST]); s0im = din("s0im", [NA, G, NST])
    ptab = din("ptab", [1, NPAGES], I32)
    flag = din("flag", [128, 3])
    cache_k = din("cache_k", [1280 * 128, D]); cache_v = din("cache_v", [1280 * 128, D])
    ada_w = din("ada_w", [DEPTH, D, 6 * D]); ada_b = din("ada_b", [DEPTH, 6 * D])
    norm_w = din("norm_w", [DEPTH, 2, D])
    mlp_up = din("mlp_up", [DEPTH, D, DFF]); mlp_down = din("mlp_down", [DEPTH, DFF, D])
    lam_re = din("ssm_lambda_re", [NA, G, NST]); lam_im = din("ssm_lambda_im", [NA, G, NST])
    log_step = din("ssm_log_step", [NA, G])
    b_re = din("ssm_b_re", [NA, G, NST, 16]); b_im = din("ssm_b_im", [NA, G, NST, 16])
    c_re = din("ssm_c_re", [NA, G, 16, NST]); c_im = din("ssm_c_im", [NA, G, 16, NST])
    ssm_d = din("ssm_d", [NA, D]); glu_w = din("glu_w", [NA, D, 2 * D])
    kv_ada_w = din("kv_ada_w", [D, 2 * D]); kv_ada_b = din("kv_ada_b", [1, 2 * D])
    kv_norm_w = din("kv_norm_w", [1, D]); kv_w = din("kv_w", [D, 2 * D])
    k_norm_w = din("k_norm_w", [1, 128])
    attn_wq = din("attn_wq", [2, D, D]); q_norm_w = din("q_norm_w", [2, 128])
    lq1 = din("lambda_q1", [2, 128]); lk1 = din("lambda_k1", [2, 128])
    lq2 = din("lambda_q2", [2, 128]); lk2 = din("lambda_k2", [2, 128])
    subln_w = din("subln_w", [2, 256]); attn_wo = din("attn_wo", [2, D, D])
    cident = din("cident", [128, 128]); ctri = din("ctri", [128, 128]); ccaus = din("ccaus", [128, 128])
    y_o = dout("y_o", [TT, D]); y_s = dout("y_s", [4, D])
    sp_re = dout("sp_re", [NA, G, NST]); sp_im = dout("sp_im", [NA, G, NST])
    k_o = dout("k_o", [TT, D]); v_o = dout("v_o", [TT, D])
    ss_re = dout("ss_re", [NA, G, NST]); ss_im = dout("ss_im", [NA, G, NST])
    k_s = dout("k_s", [4, D]); v_s = dout("v_s", [4, D])
    modD = dscr("modD", [2, 4 * 6 * D + 2 * D])
    tabD = dscr("tabD", [NA, 6, G, 128, 128], BF16)
    ktD = dscr("ktD", [16, 128, 2 * TT], BF16)
    vD = dscr("vD", [2 * TT, D], BF16)
    sfinD = dscr("sfinD", [NA, 2, 128, G])

    sb = lambda name, shape, dt: stack.enter_context(nc.sbuf_tensor(name, list(shape), dt))
    X = sb("X", [128, 8, D], F32)
    XS = sb("XS", [128, D], F32)
    HT = sb("HT", [128, 9, D], BF16)
    HF = sb("HF", [128, NCT, TT + 4], BF16)
    WB = sb("WB", [128, 2, NCT, 512], BF16)
    HID = sb("HID", [128, 4, TT + 8], BF16)
    MV = sb("MV", [128, D], F32)
    IDB = sb("IDB", [128, 128], BF16)
    IDF = sb("IDF", [128, 128], F32)
    TRI = sb("TRI", [128, 128], BF16)
    CAUS = sb("CAUS", [128, 128], BF16)
    FLG = sb("FLG", [128, 3], F32)
    SM = sb("SM", [128, 64], F32)
    CT = sb("CT", [128, NCT, 2], BF16)
    PS = [stack.enter_context(nc.psum_tensor("ps%d" % i, [128, 512], F32)) for i in range(8)]

    def psbf(i):
        return PS[i][:].bitcast(BF16)

    with nc.allow_non_contiguous_dma(reason="small setup loads"):
        pass
    P.dma("sp", IDF[:], cident[:, :], w=["IDF"])
    P.dma("pool", TRI[:], ctri[:, :], w=["TRI"])
    P.dma("pool", CAUS[:], ccaus[:, :], w=["CAUS"])
    P.dma("sp", FLG[:], flag[:, :], w=["FLG"])
    P.op("dve", lambda e: e.tensor_copy(out=IDB[:], in_=IDF[:]), r=["IDF"], w=["IDB"])

    wslot = [0]

    def load_w(src_ap, kt, ncols):
        s = wslot[0]; wslot[0] ^= 1
        P.dma("pool", WB[:, s, 0:kt, 0:ncols], src_ap.rearrange("(k p) n -> p k n", p=128),
              w=[("WB", s)])
        return s

    def tok_cols(ti):
        if ti < 8:
            return HF[:, :, 0:TT].rearrange("p c (k j) -> p c k j", j=8)[:, :, :, ti]
        return HF[:, :, TT:TT + 4]

    def ntok(ti):
        return 128 if ti < 8 else 4

    def linear_tok(wsrc, ncols_total, tiles, evac, kt=NCT, src=None, blk=512):
        for c0 in range(0, ncols_total, blk):
            s = load_w(wsrc[:, c0:c0 + blk], kt, blk)
            for ti in tiles:
                bank = (ti + c0 // blk) % 4
                for k in range(kt):
                    lhs = tok_cols(ti)[:, k] if src is None else src(ti, k)
                    P.op("pe", (lambda e, lhs=lhs, k=k, s=s, bank=bank, ti=ti:
                                e.matmul(PS[bank][0:ntok(ti), 0:blk], lhsT=lhs, rhs=WB[:, s, k, 0:blk],
                                         start=(k == 0), stop=(k == kt - 1))),
                         r=[("WB", s), "HF"], w=[("PS", bank)])
                evac(ti, c0, blk, bank)

    def to_feature_major(ti, src_tile_ap, dstF=None, ncol=NCT):
        n = ntok(ti)
        for c4 in range(0, ncol, 8):
            bank = 4 + (c4 // 8) % 2
            pv = psbf(bank)
            for c in range(c4, min(c4 + 8, ncol)):
                P.op("pe", (lambda e, c=c, pv=pv, c4=c4:
                            e.transpose(out=pv[:, (c - c4) * 128:(c - c4) * 128 + n],
                                        in_=src_tile_ap[:, c * 128:(c + 1) * 128], identity=IDB[0:n, 0:n])),
                     r=["IDB", "HT"], w=[("PS", bank)])
            nc8 = min(8, ncol - c4)
            dst = (tok_cols(ti) if dstF is None else dstF)[:, c4:c4 + nc8]
            src_v = pv[:, 0:nc8 * 128].rearrange("p (c k) -> p c k", k=128)[:, :, 0:n]
            if (c4 // 8) % 2:
                P.op("act", (lambda e, dst=dst, src_v=src_v: e.copy(out=dst, in_=src_v)), r=[("PS", bank)], w=["HF"])
            else:
                P.op("dve", (lambda e, dst=dst, src_v=src_v: e.tensor_copy(out=dst, in_=src_v)), r=[("PS", bank)], w=["HF"])

    def bcast_row(dst, src_row_ap, key="MV"):
        P.dma("sp", dst, src_row_ap.partition_broadcast(128)[:, 0, :], w=[key])

    def xtile(ti):
        return X[:, ti, :] if ti < 8 else XS[0:4, :]

    def rms_modulate(tiles_rows, nw_row, sc_off, sh_off):
        for tiles, row in tiles_rows:
            bcast_row(MV[:, :], modD[row:row + 1, sc_off:sc_off + D])
            for hh in range(2):
                bcast_row(XTMP[:, :], nw_row[:, hh * 1024:(hh + 1) * 1024], key="XTMP")
                P.op("dve", (lambda e, hh=hh: e.scalar_tensor_tensor(
                    out=MV[:, hh * 1024:(hh + 1) * 1024], in0=MV[:, hh * 1024:(hh + 1) * 1024], scalar=1.0,
                    in1=XTMP[:, :], op0=ALU.add, op1=ALU.mult)), r=["XTMP"], w=["MV"])
            for ti in tiles:
                n = ntok(ti)
                xt = xtile(ti)
                P.op("dve", (lambda e, n=n, ti=ti: e.memset(SM[0:n, ti:ti + 1], 0.0)), r=[], w=["SM"])
                P.op("act", (lambda e, xt=xt, n=n, ti=ti:
                             e.activation(out=HT[0:n, ti, :], in_=xt, func=AF.Square,
                                          accum_out=SM[0:n, ti:ti + 1])),
                     r=["X"], w=["HT", "SM"])
                P.op("dve", (lambda e, n=n, ti=ti:
                             e.tensor_scalar(out=SM[0:n, ti:ti + 1], in0=SM[0:n, ti:ti + 1], scalar1=1.0 / D,
                                             scalar2=EPS, op0=ALU.mult, op1=ALU.add)), r=[], w=["SM"])
                P.op("act", (lambda e, n=n, ti=ti:
                             e.activation(out=SM[0:n, ti:ti + 1], in_=SM[0:n, ti:ti + 1], func=AF.Sqrt)),
                     r=[], w=["SM"])
                P.op("dve", (lambda e, n=n, ti=ti:
                             e.reciprocal(out=SM[0:n, ti:ti + 1], in_=SM[0:n, ti:ti + 1])), r=[], w=["SM"])
                P.op("dve", (lambda e, xt=xt, n=n, ti=ti:
                             e.scalar_tensor_tensor(out=HT[0:n, ti, :], in0=xt, scalar=SM[0:n, ti:ti + 1],
                                                    in1=MV[0:n, :], op0=ALU.mult, op1=ALU.mult)),
                     r=["X", "MV", "SM"], w=["HT"])
            bcast_row(MV[:, :], modD[row:row + 1, sh_off:sh_off + D])
            for ti in tiles:
                n = ntok(ti)
                P.op("pool", (lambda e, n=n, ti=ti:
                              e.tensor_tensor(out=HT[0:n, ti, :], in0=HT[0:n, ti, :], in1=MV[0:n, :],
                                              op=ALU.add)), r=["MV"], w=["HT"])

    XTMP = sb("XTMP", [128, 1024], F32)
    XB16 = XTMP[:, :].bitcast(BF16).rearrange("p (g j c) -> p g j c", j=8, c=16)

    coefD = dscr("coefD", [NA, 128, 6 * 128])
    dkD = dscr("dkD", [NA, 128, 128])
    SFIN = sb("SFIN", [128, 2, 128], F32)
    SINIT = SFIN
    S0T = sb("S0T", [128, 2, 128], F32)
    SSF = S0T
    SCUR = sb("SCUR", [128, 2, 2, 32], F32)
    STMP = sb("STMP", [128, 4, 32], F32)
    USAMP = sb("USAMP", [128, 128], BF16)
    YS = sb("YS", [128, 128], BF16)
    MT = XTMP[0:2, 0:512]
    MB = XTMP[0:2, 512:1024]
    hsD = dscr("hsD", [4, D], BF16)
    ysD = dscr("ysD", [4, D], BF16)

    for r_ in range(2):
        P.dma("pool", CT[:, :, r_], cvec[r_:r_ + 1, :].rearrange("o (c p) -> p (o c)", p=128), w=["CT"],
              allow_slow_non_contiguous=True)
    mod_specs = [(ada_w[l], ada_b[l:l + 1, :], 6 * D, l * 6 * D) for l in range(DEPTH)]
    mod_specs.append((kv_ada_w, kv_ada_b, 2 * D, 24 * D))
    def mod_phase():
      for wsrc, bsrc, ncol, off in mod_specs:
        for c0 in range(0, ncol, 512):
            held = P.deferred; P.deferred = None
            mod_block(wsrc, bsrc, off, c0)
            P.deferred = held
            P.flush(8)

    def mod_block(wsrc, bsrc, off, c0):
            s = load_w(wsrc[:, c0:c0 + 512], NCT, 512)
            P.dma("sp", MB, bsrc[:, c0:c0 + 512].partition_broadcast(2)[:, 0, :], w=["MB"])
            for k in range(NCT):
                P.op("pe", (lambda e, k=k, s=s: e.matmul(PS[0][0:2, 0:512], lhsT=CT[:, k, :], rhs=WB[:, s, k, :],
                                                         start=(k == 0), stop=(k == NCT - 1))),
                     r=[("WB", s), "CT"], w=[("PS", 0)])
            P.op("dve", lambda e: e.tensor_tensor(out=MT, in0=PS[0][0:2, 0:512], in1=MB, op=ALU.add),
                 r=[("PS", 0), "MB"], w=["MT"])
            P.dma("sp", modD[:, off + c0:off + c0 + 512], MT, r=["MT"], w=["modD"])

    def xs_(i, a, b):
        return X[:, i, a:b]
    BR = X[:, 0, 0:1024].rearrange("p (n c) -> p n c", c=16); BI = X[:, 0, 1024:2048].rearrange("p (n c) -> p n c", c=16)
    CR = X[:, 1, 0:1024].rearrange("p (c n) -> p c n", n=64); CI = X[:, 1, 1024:2048].rearrange("p (c n) -> p c n", n=64)
    BbR = X[:, 2, 0:1024].rearrange("p (n c) -> p n c", c=16); BbI = X[:, 2, 1024:2048].rearrange("p (n c) -> p n c", c=16)
    ANG = X[:, 3, 0:1024]; RR = X[:, 3, 1024:2048]
    MAG = X[:, 4, 0:1024].rearrange("p (d n) -> p d n", n=64); RI = X[:, 4, 1024:2048].bitcast(I32)
    PR = X[:, 5, 0:1024].rearrange("p (d n) -> p d n", n=64); PIm = X[:, 5, 1024:2048].rearrange("p (d n) -> p d n", n=64)
    T1 = X[:, 6, 0:1024].rearrange("p (n c) -> p n c", c=16); T2 = X[:, 6, 1024:2048].rearrange("p (n c) -> p n c", c=16)
    LR = X[:, 7, 0:64]; LI = X[:, 7, 64:128]; LS = X[:, 7, 128:129]; DT = X[:, 7, 129:130]
    LRDT = X[:, 7, 192:256]; TH = X[:, 7, 256:320]; FRE = X[:, 7, 320:384]; FIM = X[:, 7, 384:448]
    DEN = X[:, 7, 448:512]; XR = X[:, 7, 512:576]; TA = X[:, 7, 576:640]; TB_ = X[:, 7, 640:704]
    TR2 = X[:, 7, 768:1024]
    TABT = HT[:, 0:8, :]
    K = "PREP"

    def dve(fn, r=(K,), w=(K,)):
        P.op("dve", fn, r=list(r), w=list(w))

    def didx(d):
        return d + 7

    P.deferred = []
    for l in range(NA):
        P.dma("sp", LR, lam_re[l], w=[K]); P.dma("sp", LI, lam_im[l], w=[K])
        P.dma("sp", LS, log_step[l:l + 1, :].rearrange("o g -> g o"), w=[K], allow_slow_non_contiguous=True)
        P.dma("sp", BR, b_re[l], w=[K]); P.dma("sp", BI, b_im[l], w=[K])
        P.dma("sp", CR, c_re[l], w=[K]); P.dma("sp", CI, c_im[l], w=[K])
        P.op("act", lambda e: e.activation(out=DT, in_=LS, func=AF.Exp), r=[K], w=[K])
        dve(lambda e: e.tensor_scalar(out=LRDT, in0=LR, scalar1=DT, scalar2=None, op0=ALU.mult))
        dve(lambda e: e.tensor_scalar(out=TH, in0=LI, scalar1=DT, scalar2=None, op0=ALU.mult))
        for di in range(16):
            d = float(di - 7)
            P.op("act", (lambda e, di=di, d=d: e.activation(out=MAG[:, di, :], in_=LRDT, func=AF.Exp, scale=d)),
                 r=[K], w=[K])
            dve(lambda e, di=di, d=d: e.tensor_scalar(out=ANG[:, di * 64:(di + 1) * 64], in0=TH,
                                                      scalar1=d / TWO_PI, scalar2=None, op0=ALU.mult))
        for which, dst in ((0.0, PIm), (0.25, PR)):
            dve(lambda e, which=which: e.tensor_scalar(out=RR, in0=ANG, scalar1=which, scalar2=None, op0=ALU.add))
            dve(lambda e: e.tensor_copy(out=RI, in_=RR))
            dve(lambda e: e.tensor_copy(out=X[:, 6, 0:1024], in_=RI))
            dve(lambda e: e.tensor_tensor(out=RR, in0=RR, in1=X[:, 6, 0:1024], op=ALU.subtract))
            dve(lambda e: e.tensor_scalar(out=X[:, 6, 0:1024], in0=RR, scalar1=0.5, scalar2=None, op0=ALU.is_gt))
            dve(lambda e: e.tensor_tensor(out=RR, in0=RR, in1=X[:, 6, 0:1024], op=ALU.subtract))
            dve(lambda e: e.tensor_scalar(out=X[:, 6, 0:1024], in0=RR, scalar1=-0.5, scalar2=None, op0=ALU.is_lt))
            dve(lambda e: e.tensor_tensor(out=RR, in0=RR, in1=X[:, 6, 0:1024], op=ALU.add))
            P.op("act", (lambda e, dst=dst: e.activation(out=dst.rearrange("p d n -> p (d n)"), in_=RR, func=AF.Sin,
                                                         scale=TWO_PI)), r=[K], w=[K])
            dve(lambda e, dst=dst: e.tensor_tensor(out=dst, in0=dst, in1=MAG, op=ALU.mult))
        a_re = PR[:, didx(1), :]; a_im = PIm[:, didx(1), :]
        dve(lambda e: e.tensor_scalar(out=XR, in0=a_re, scalar1=-1.0, scalar2=None, op0=ALU.add))
        dve(lambda e: e.tensor_tensor(out=DEN, in0=LR, in1=LR, op=ALU.mult))
        dve(lambda e: e.tensor_tensor(out=TA, in0=LI, in1=LI, op=ALU.mult))
        dve(lambda e: e.tensor_tensor(out=DEN, in0=DEN, in1=TA, op=ALU.add))
        dve(lambda e: e.reciprocal(out=DEN, in_=DEN))
        dve(lambda e: e.tensor_tensor(out=TA, in0=XR, in1=LR, op=ALU.mult))
        dve(lambda e: e.tensor_tensor(out=TB_, in0=a_im, in1=LI, op=ALU.mult))
        dve(lambda e: e.tensor_tensor(out=TA, in0=TA, in1=TB_, op=ALU.add))
        dve(lambda e: e.tensor_tensor(out=FRE, in0=TA, in1=DEN, op=ALU.mult))
        dve(lambda e: e.tensor_tensor(out=TA, in0=a_im, in1=LR, op=ALU.mult))
        dve(lambda e: e.tensor_tensor(out=TB_, in0=XR, in1=LI, op=ALU.mult))
        dve(lambda e: e.tensor_tensor(out=TA, in0=TA, in1=TB_, op=ALU.subtract))
        dve(lambda e: e.tensor_tensor(out=FIM, in0=TA, in1=DEN, op=ALU.mult))
        fre_b = FRE.unsqueeze(2).broadcast_to([128, 64, 16]); fim_b = FIM.unsqueeze(2).broadcast_to([128, 64, 16])
        dve(lambda e: e.tensor_tensor(out=T1, in0=BR, in1=fre_b, op=ALU.mult))
        dve(lambda e: e.tensor_tensor(out=T2, in0=BI, in1=fim_b, op=ALU.mult))
        dve(lambda e: e.tensor_tensor(out=BbR, in0=T1, in1=T2, op=ALU.subtract))
        dve(lambda e: e.tensor_tensor(out=T1, in0=BI, in1=fre_b, op=ALU.mult))
        dve(lambda e: e.tensor_tensor(out=T2, in0=BR, in1=fim_b, op=ALU.mult))
        dve(lambda e: e.tensor_tensor(out=BbI, in0=T1, in1=T2, op=ALU.add))
        CRn = CR.rearrange("p c n -> p n c"); CIn = CI.rearrange("p c n -> p n c")
        kinds = [(BbR, BbI, lambda j: -j, 0, False, False),
                 (CRn, CIn, lambda j: j, 0, False, True),
                 (CRn, CIn, lambda j: j + 1, 0, False, True),
                 (BbR, BbI, lambda j: 7 - j, 1, False, False),
                 (BbR, BbI, lambda j: 7 - j, 1, True, False)]
        for kind, (sR, sI, dj, layout, swap, negim) in enumerate(kinds):
            if layout == 0:
                TV = TABT.rearrange("p a b -> p (a b)").rearrange("p (m j c) -> p m j c", j=8, c=16)
            else:
                TV = TABT.rearrange("p a b -> p (a b)").rearrange("p (j c m) -> p j c m", c=16, m=128)
            for j in range(8):
                di = didx(dj(j))
                prb = PR[:, di, :].unsqueeze(2).broadcast_to([128, 64, 16])
                pib = PIm[:, di, :].unsqueeze(2).broadcast_to([128, 64, 16])
                for h in range(2):
                    hh = (1 - h) if swap else h
                    if layout == 0:
                        outv = TV[:, hh * 64:(hh + 1) * 64, j, :]
                    else:
                        outv = TV[:, j, :, hh * 64:(hh + 1) * 64].rearrange("p c n -> p n c")
                    if not negim:
                        a0, b0, opx = (prb, pib, ALU.subtract) if h == 0 else (pib, prb, ALU.add)
                        dve(lambda e, a0=a0: e.tensor_tensor(out=T1, in0=sR, in1=a0, op=ALU.mult))
                        dve(lambda e, b0=b0: e.tensor_tensor(out=T2, in0=sI, in1=b0, op=ALU.mult))
                        P.op("pool", (lambda e, outv=outv, opx=opx: e.tensor_tensor(out=outv, in0=T1, in1=T2, op=opx)),
                             r=[K], w=[K, "HT"])
                    else:
                        if h == 0:
                            dve(lambda e: e.tensor_tensor(out=T1, in0=sR, in1=prb, op=ALU.mult))
                            dve(lambda e: e.tensor_tensor(out=T2, in0=sI, in1=pib, op=ALU.mult))
                            P.op("pool", (lambda e, outv=outv: e.tensor_tensor(out=outv, in0=T1, in1=T2, op=ALU.subtract)),
                                 r=[K], w=[K, "HT"])
                        else:
                            dve(lambda e: e.tensor_tensor(out=T1, in0=sR, in1=pib, op=ALU.mult))
                            dve(lambda e: e.tensor_tensor(out=T2, in0=sI, in1=prb, op=ALU.mult))
                            dve(lambda e: e.tensor_tensor(out=T1, in0=T1, in1=T2, op=ALU.add))
                            P.op("pool", (lambda e, outv=outv: e.tensor_scalar(out=outv, in0=T1, scalar1=-1.0, scalar2=None,
                                                                               op0=ALU.mult)), r=[K], w=[K, "HT"])
            P.dma("sp", tabD[l, kind].rearrange("g a b -> g (a b)"), TABT.rearrange("p a b -> p (a b)"),
                  r=[K, "HT"], w=["tabD"])
        TR2v = TR2
        specs = [(8, 1.0, 1.0, PR), (8, -1.0, 1.0, PIm), (8, 1.0, -1.0, PIm),
                 (-4, 1.0, 1.0, PR), (-4, -1.0, 1.0, PIm), (-4, 1.0, -1.0, PIm)]
        for ci, (dd, s0, s1, srcT) in enumerate(specs):
            dve(lambda e, dd=dd, s0=s0, srcT=srcT: e.tensor_scalar(out=TR2v[:, 0:64], in0=srcT[:, didx(dd), :], scalar1=s0,
                                                                    scalar2=None, op0=ALU.mult))
            dve(lambda e, dd=dd, s1=s1, srcT=srcT: e.tensor_scalar(out=TR2v[:, 64:128], in0=srcT[:, didx(dd), :], scalar1=s1,
                                                                    scalar2=None, op0=ALU.mult))
            P.op("pe", lambda e: e.transpose(out=PS[7][:, 0:128], in_=TR2v[:, 0:128], identity=IDF[:]),
                 r=[K, "IDF"], w=[("PS", 7)])
            P.op("act", (lambda e, ci=ci: e.copy(out=X[:, 7, 1024 + ci * 128:1024 + (ci + 1) * 128], in_=PS[7][:, 0:128])),
                 r=[("PS", 7)], w=[K])
        P.dma("sp", coefD[l], X[:, 7, 1024:1792], r=[K], w=["coefD"])
        P.dma("sp", X[0:16, 7, 1920:2048], ssm_d[l:l + 1, :].rearrange("o (g c) -> c (o g)", c=16), w=[K],
              allow_slow_non_contiguous=True)
        for j in range(8):
            P.dma("sp", dkD[l, j * 16:(j + 1) * 16, :], X[0:16, 7, 1920:2048], r=[K], w=["dkD"])
    print("deferred prep ops", len(P.deferred))
    mod_phase()
    P.flush()
    P.barrier()
    GSV = sb("GSV", [4, D], F32)
    WBf = WB[:].rearrange("p s k n -> p (s k n)")
    DDv = WBf[:, 0:4128].rearrange("p (g k) -> p g k", k=129)
    DSv = WBf[:, 4128:8256].rearrange("p (g k) -> p g k", k=129)
    SAv = WBf[:, 8256:12384].rearrange("p (g k) -> p g k", k=129)
    TB2 = WBf[:, 12384:13664].rearrange("p (s t c) -> p s t c", s=2, t=5)
    TSB = WBf[:, 13664:13792]
    YSB = WBf[:, 13792:13922]
    COEFL = WBf[:, 13924:15460].bitcast(F32).rearrange("p (a g) -> p a g", g=128)
    DKL = WBf[:, 15460:15716].bitcast(F32)
    UAv = HID[:].rearrange("p a b -> p (a b)")[:, 0:4128].rearrange("p (g k) -> p g k", k=129)
    P.op("pool", lambda e: e.memset(USAMP[:, :], 0.0), w=["USAMP"])
    P.op("pool", lambda e: e.memset(SFIN[:, :, :], 0.0), w=["SFIN"])
    dbg = {}

    def dbg_dump(name, src_ap, shape, dt=F32, keys=()):
        if not DEBUG:
            return
        o = dout("dbg_" + name, shape, dt)
        P.dma("sp", o, src_ap, r=list(keys))
        dbg[name] = o

    def load_x(src, with_sample):
        sv = src.rearrange("(k j) d -> k j d", j=8)
        for j in range(8):
            P.dma("sp", X[:, j, :], sv[:, j, :], w=["X"])
        if with_sample:
            P.dma("sp", XS[0:4, :], xs[:, :], w=["X"])

    def ssm_pass(l, own):
        tiles = list(range(8))
        if own:
            P.op("pool", lambda e: e.memset(USAMP[:, :], 0.0), w=["USAMP"])
            P.dma("sp", hsD[:, :], HT[0:4, 8, :], r=["HT"], w=["hsD"])
            for j in range(4):
                P.dma("sp", USAMP[j * 16:(j + 1) * 16, :], hsD[j:j + 1, :].rearrange("o (g c) -> c (o g)", c=16),
                      r=["hsD"], w=["USAMP"], allow_slow_non_contiguous=True)
            P.dma("sp", S0T[0:64, 0, :], s0re[l].rearrange("g n -> n g"), w=["S0T"], allow_slow_non_contiguous=True)
            P.dma("sp", S0T[64:128, 0, :], s0im[l].rearrange("g n -> n g"), w=["S0T"], allow_slow_non_contiguous=True)
            P.dma("sp", S0T[0:64, 1, :], s0im[l].rearrange("g n -> n g"), w=["S0T"], allow_slow_non_contiguous=True)
            P.dma("sp", S0T[64:128, 1, :], s0re[l].rearrange("g n -> n g"), w=["S0T"], allow_slow_non_contiguous=True)
            P.dma("sp", SFIN[:, :, :], sfinD[l].rearrange("s p g -> p s g"), r=["sfinD"], w=["SFIN"])
            P.op("dve", lambda e: e.tensor_scalar(out=SINIT[:, :, :], in0=SFIN[:, :, :], scalar1=FLG[:, 0:1], scalar2=None,
                                                  op0=ALU.mult), r=["SFIN", "FLG"], w=["SINIT", "SFIN"])
        else:
            P.op("dve", lambda e: e.memset(SINIT[:, :, :], 0.0), w=["SINIT", "SFIN"])
        P.dma("sp", COEFL, coefD[l].rearrange("p (a g) -> p a g", g=128), w=["COEFL"])
        P.dma("sp", DKL, dkD[l], w=["DKL"])
        P.barrier()
        A8 = COEFL[:, 0, :]; Bc8 = COEFL[:, 1, :]; Bcs8 = COEFL[:, 2, :]
        A4 = COEFL[:, 3, :]; Bc4 = COEFL[:, 4, :]
        for rnd in range(4):
            g0 = rnd * 32
            for gi in range(32):
                g = g0 + gi
                slot = gi % 2
                P.dma("sp", TB2[:, slot, :, :], tabD[l, 0:5, g].rearrange("t p c -> p t c"), w=[("TB2", slot)])
                ub = 4 + gi % 2
                if gi % 16 == 0:
                    P.op("pool", (lambda e, g=g: e.tensor_copy(
                        out=XB16, in_=HT[:, 0:8, 16 * g:16 * g + 256].rearrange("p j (g c) -> p g j c", c=16))),
                        r=["HT"], w=["XTMP"])
                P.op("pe", (lambda e, gi=gi, ub=ub: e.transpose(out=psbf(ub)[:, 0:128],
                                                              in_=XB16[:, gi % 16].rearrange("p j c -> p (j c)"),
                                                              identity=IDB[:])), r=["XTMP", "IDB"], w=[("PS", ub)])
                P.op("act", (lambda e, gi=gi, ub=ub: e.copy(out=UAv[:, gi, 0:128], in_=psbf(ub)[:, 0:128])),
                     r=[("PS", ub)], w=[("UA", gi)])
                if own:
                    P.op("pool", (lambda e, gi=gi, g=g: e.tensor_copy(out=UAv[:, gi, 128:129], in_=USAMP[:, g:g + 1])),
                         r=["USAMP"], w=[("UA", gi)])
                else:
                    P.op("pool", (lambda e, gi=gi: e.memset(UAv[:, gi, 128:129], 0.0)), w=[("UA", gi)])
                b0 = gi % 2; b1 = 2 + gi % 2
                P.op("pe", (lambda e, gi=gi, slot=slot, b0=b0: e.matmul(PS[b0][:, 0:129], lhsT=TB2[:, slot, 3, :], rhs=UAv[:, gi, :],
                                                                        start=True, stop=True)),
                     r=[("TB2", slot), ("UA", gi)], w=[("PS", b0)])
                P.op("pe", (lambda e, gi=gi, slot=slot, b1=b1: e.matmul(PS[b1][:, 0:129], lhsT=TB2[:, slot, 4, :], rhs=UAv[:, gi, :],
                                                                        start=True, stop=True)),
                     r=[("TB2", slot), ("UA", gi)], w=[("PS", b1)])
                P.op("dve", (lambda e, gi=gi, b0=b0: e.tensor_copy(out=DDv[:, gi, :], in_=PS[b0][:, 0:129])),
                     r=[("PS", b0)], w=["DD"])
                P.op("act", (lambda e, gi=gi, b1=b1: e.copy(out=DSv[:, gi, :], in_=PS[b1][:, 0:129])),
                     r=[("PS", b1)], w=["DS"])
            P.barrier()
            EN = "dve"
            a8 = A8[:, g0:g0 + 32]; bc8 = Bc8[:, g0:g0 + 32]; bcs8 = Bcs8[:, g0:g0 + 32]
            P.op(EN, (lambda e, g0=g0: e.tensor_copy(out=SCUR[:, 0, :, :], in_=SINIT[:, :, g0:g0 + 32])),
                 r=["SINIT"], w=["SC"])
            AA2 = SM[:, 0:64].rearrange("p (s g) -> p s g", s=2)
            for s_ in range(2):
                P.op(EN, (lambda e, s_=s_: e.tensor_copy(out=AA2[:, s_, :], in_=a8)), r=["COEFL"], w=["AA2"])
            D2 = WBf[:, 0:8256].rearrange("p (s g k) -> p s g k", s=2, k=129)
            T0 = STMP[:, 0:2, :]; T1 = STMP[:, 2:4, :]
            for k in range(128):
                pa = k % 2; pb = 1 - pa
                Z = SCUR[:, pa, :, :]
                S = SCUR[:, pa, 0, :]; Sw = SCUR[:, pa, 1, :]
                P.op("act", (lambda e: e.copy(out=SAv[:, :, k], in_=S)), r=["SC"], w=["SA"])
                P.op(EN, (lambda e: e.tensor_tensor(out=T0, in0=Z, in1=AA2, op=ALU.mult)), r=["SC", "AA2"], w=["ST0"])
                P.op(EN, (lambda e: e.tensor_tensor(out=T1[:, 0, :], in0=Sw, in1=bc8, op=ALU.mult)), r=["SC"], w=["ST1"])
                P.op(EN, (lambda e: e.tensor_tensor(out=T1[:, 1, :], in0=S, in1=bcs8, op=ALU.mult)), r=["SC"], w=["ST1"])
                P.op(EN, (lambda e: e.tensor_tensor(out=T0, in0=T0, in1=D2[:, :, :, k], op=ALU.add)), r=["DD", "DS"], w=["ST0"])
                P.op(EN, (lambda e: e.tensor_tensor(out=SCUR[:, pb, :, :], in0=T0, in1=T1, op=ALU.add)),
                     r=["ST0", "ST1"], w=["SC"])
            P.op("dve", (lambda e, g0=g0: e.tensor_copy(out=SFIN[:, :, g0:g0 + 32], in_=SCUR[:, 0, :, :])), r=["SC"], w=["SFIN"])
            if own:
                a4 = A4[:, g0:g0 + 32]; bc4 = Bc4[:, g0:g0 + 32]
                s0 = S0T[:, 0, g0:g0 + 32]; s0w = S0T[:, 1, g0:g0 + 32]
                P.op("act", (lambda e, s0=s0: e.copy(out=SAv[:, :, 128], in_=s0)), r=["S0T"], w=["SA"])
                P.op("dve", (lambda e, s0=s0: e.tensor_tensor(out=STMP[:, 0, :], in0=s0, in1=a8, op=ALU.mult)), r=["S0T"], w=["ST0"])
                P.op("dve", (lambda e, s0w=s0w: e.tensor_tensor(out=STMP[:, 1, :], in0=s0w, in1=bc8, op=ALU.mult)), r=["S0T"], w=["ST1", "ST0"])
                P.op("dve", lambda e: e.tensor_tensor(out=STMP[:, 0, :], in0=STMP[:, 0, :], in1=STMP[:, 1, :], op=ALU.add), r=["ST1"], w=["ST0"])
                P.op("dve", lambda e: e.tensor_tensor(out=STMP[:, 0, :], in0=STMP[:, 0, :], in1=DDv[:, :, 128], op=ALU.add), r=["DD"], w=["ST0"])
                P.op("dve", (lambda e, s0w=s0w: e.tensor_tensor(out=STMP[:, 2, :], in0=s0w, in1=a8, op=ALU.mult)), r=["S0T"], w=["ST2", "ST1"])
                P.op("dve", (lambda e, s0=s0: e.tensor_tensor(out=STMP[:, 3, :], in0=s0, in1=bcs8, op=ALU.mult)), r=["S0T"], w=["ST3", "ST1"])
                P.op("dve", lambda e: e.tensor_tensor(out=STMP[:, 2, :], in0=STMP[:, 2, :], in1=STMP[:, 3, :], op=ALU.add), r=["ST3"], w=["ST2"])
                P.op("dve", lambda e: e.tensor_tensor(out=STMP[:, 2, :], in0=STMP[:, 2, :], in1=DSv[:, :, 128], op=ALU.add), r=["DS"], w=["ST2"])
                P.op("dve", lambda e: e.tensor_tensor(out=STMP[:, 0, :], in0=STMP[:, 0, :], in1=a4, op=ALU.mult), r=[], w=["ST0"])
                P.op("dve", lambda e: e.tensor_tensor(out=STMP[:, 2, :], in0=STMP[:, 2, :], in1=bc4, op=ALU.mult), r=[], w=["ST2"])
                P.op("dve", (lambda e, g0=g0: e.tensor_tensor(out=SSF[:, 0, g0:g0 + 32], in0=STMP[:, 0, :], in1=STMP[:, 2, :], op=ALU.add)),
                     r=["ST0", "ST2"], w=["SSF"])
            P.barrier()
            for gi in range(32):
                g = g0 + gi
                slot = gi % 2
                P.dma("sp", TB2[:, slot, 0:3, :], tabD[l, 0:3, g].rearrange("t p c -> p t c"), w=[("TB2", slot)])
                P.op("pe", (lambda e, slot=slot: e.matmul(PS[6][:, 0:128], lhsT=TB2[:, slot, 0, :], rhs=TB2[:, slot, 1, :],
                                                          start=True, stop=True)), r=[("TB2", slot)], w=[("PS", 6)])
                P.op("dve", lambda e: e.tensor_tensor(out=TSB, in0=PS[6][:, 0:128], in1=TRI[:, :], op=ALU.mult),
                     r=[("PS", 6), "TRI"], w=["TSB"])
                P.op("dve", (lambda e, g=g: e.scalar_tensor_tensor(out=TSB, in0=IDF[:, :], scalar=DKL[:, g:g + 1], in1=TSB,
                                                                   op0=ALU.mult, op1=ALU.add)), r=["IDF"], w=["TSB"])
                P.op("pe", (lambda e, gi=gi: e.matmul(PS[7][:, 0:129], lhsT=TSB, rhs=UAv[:, gi, :], start=True, stop=False)),
                     r=["TSB", ("UA", gi)], w=[("PS", 7)])
                P.op("pe", (lambda e, gi=gi, slot=slot: e.matmul(PS[7][:, 0:129], lhsT=TB2[:, slot, 2, :], rhs=SAv[:, gi, :],
                                                                 start=False, stop=True)), r=[("TB2", slot), "SA"], w=[("PS", 7)])
                P.op("act", lambda e: e.copy(out=YSB[:, 0:129], in_=PS[7][:, 0:129]), r=[("PS", 7)], w=["YSB"])
                yb = 4 + gi % 2
                P.op("pe", (lambda e, yb=yb: e.transpose(out=psbf(yb)[:, 0:128], in_=YSB[:, 0:128], identity=IDB[:])),
                     r=["YSB", "IDB"], w=[("PS", yb)])
                P.op("dve", (lambda e, yb=yb, g=g: e.tensor_copy(out=HT[:, 0:8, 16 * g:16 * g + 16],
                                                                 in_=psbf(yb)[:, 0:128].rearrange("p (j c) -> p j c", c=16))),
                     r=[("PS", yb)], w=["HT"])
                if own:
                    P.op("pool", (lambda e, g=g: e.tensor_copy(out=YS[:, g:g + 1], in_=YSB[:, 128:129])), r=["YSB"], w=["YS"])
            P.barrier()
        if own:
            for j in range(4):
                P.dma("sp", ysD[j:j + 1, :].rearrange("o (g c) -> c (o g)", c=16), YS[j * 16:(j + 1) * 16, :],
                      r=["YS"], w=["ysD"], allow_slow_non_contiguous=True)
            P.dma("sp", HT[0:4, 8, :], ysD[:, :], r=["ysD"], w=["HT"])
            P.dma("sp", sp_re[l].rearrange("g n -> n g"), SFIN[0:64, 0, :], r=["SFIN"], allow_slow_non_contiguous=True)
            P.dma("sp", sp_im[l].rearrange("g n -> n g"), SFIN[64:128, 0, :], r=["SFIN"], allow_slow_non_contiguous=True)
            P.dma("sp", ss_re[l].rearrange("g n -> n g"), SSF[0:64, 0, :], r=["SSF"], allow_slow_non_contiguous=True)
            P.dma("sp", ss_im[l].rearrange("g n -> n g"), SSF[64:128, 0, :], r=["SSF"], allow_slow_non_contiguous=True)
        else:
            P.dma("sp", sfinD[l].rearrange("s p g -> p s g"), SFIN[:, :, :], r=["SFIN"], w=["sfinD"])
        P.barrier()

    def hid_cols(ti):
        if ti < 8:
            return HID[:, :, 0:TT].rearrange("p c (k j) -> p c k j", j=8)[:, :, :, ti]
        return HID[:, :, TT:TT + 4]

    def gate_vec(ti, c0, n):
        return MV[0:128, c0:c0 + n] if ti < 8 else GSV[0:4, c0:c0 + n]

    def load_gate(l, q, own):
        bcast_row(MV[:, :], modD[0:1, l * 6 * D + q * D:l * 6 * D + (q + 1) * D])
        if own:
            P.dma("sp", GSV[:, :], modD[1:2, l * 6 * D + q * D:l * 6 * D + (q + 1) * D].partition_broadcast(4)[:, 0, :], w=["GSV"])

    xh = [0]

    def resid_add(ti, c0, n, bank):
        nt = ntok(ti)
        h_ = xh[0]; xh[0] ^= 1
        tmp = XTMP[0:nt, h_ * 512:h_ * 512 + n]
        P.op("dve", (lambda e, tmp=tmp, nt=nt, ti=ti: e.tensor_tensor(out=tmp, in0=PS[bank][0:nt, 0:n], in1=gate_vec(ti, c0, n),
                                                                    op=ALU.mult)), r=[("PS", bank), "MV", "GSV"], w=[("XT", h_)])
        xt = xtile(ti)
        P.op("pool", (lambda e, tmp=tmp, xt=xt: e.tensor_tensor(out=xt[:, c0:c0 + n], in0=xt[:, c0:c0 + n], in1=tmp, op=ALU.add)),
             r=[("XT", h_)], w=["X"])

    def a_layer(l, own):
        tiles = list(range(8)) + ([8] if own else [])
        rows = [(list(range(8)), 0)] + ([([8], 1)] if own else [])
        base = l * 6 * D
        rms_modulate(rows, norm_w[l, 0:1, :], base + D, base)
        P.barrier()
        if l == 0 and own:
            for j in range(2):
                dbg_dump("h0_%d" % j, HT[:, j, :], [128, D], BF16, keys=["HT"])
        ssm_pass(l, own)
        if l == 0 and own:
            for j in range(2):
                dbg_dump("y0_%d" % j, HT[:, j, :], [128, D], BF16, keys=["HT"])
            dbg_dump("ys0", HT[0:4, 8, :], [4, D], BF16, keys=["HT"])
        for ti in tiles:
            n = ntok(ti)
            P.op("act", (lambda e, n=n, ti=ti: e.activation(out=HT[0:n, ti, :], in_=HT[0:n, ti, :], func=AF.Gelu_apprx_tanh)),
                 r=[], w=["HT"])
        for ti in tiles:
            to_feature_major(ti, HT[0:ntok(ti), ti, :])
        P.barrier()
        load_gate(l, 2, own)
        for c0 in range(0, D, 512):
            sA = load_w(glu_w[l][:, c0:c0 + 512], NCT, 512)
            sB = load_w(glu_w[l][:, D + c0:D + c0 + 512], NCT, 512)
            for ti in tiles:
                n = ntok(ti)
                bv = (2 * ti) % 4; bg = bv + 1
                for (s, bank) in ((sA, bv), (sB, bg)):
                    for k in range(NCT):
                        P.op("pe", (lambda e, k=k, s=s, bank=bank, ti=ti, n=n:
                                    e.matmul(PS[bank][0:n, 0:512], lhsT=tok_cols(ti)[:, k], rhs=WB[:, s, k, :],
                                             start=(k == 0), stop=(k == NCT - 1))), r=[("WB", s), "HF"], w=[("PS", bank)])
                h_ = xh[0]; xh[0] ^= 1
                tmp = XTMP[0:n, h_ * 512:(h_ + 1) * 512]
                P.op("act", (lambda e, tmp=tmp, bg=bg, n=n: e.activation(out=tmp, in_=PS[bg][0:n, 0:512], func=AF.Sigmoid)),
                     r=[("PS", bg)], w=[("XT", h_)])
                P.op("dve", (lambda e, tmp=tmp, bv=bv, n=n: e.tensor_tensor(out=tmp, in0=tmp, in1=PS[bv][0:n, 0:512], op=ALU.mult)),
                     r=[("PS", bv)], w=[("XT", h_)])
                P.op("dve", (lambda e, tmp=tmp, ti=ti, c0=c0: e.tensor_tensor(out=tmp, in0=tmp, in1=gate_vec(ti, c0, 512), op=ALU.mult)),
                     r=["MV", "GSV"], w=[("XT", h_)])
                xt = xtile(ti)
                P.op("pool", (lambda e, tmp=tmp, xt=xt, c0=c0: e.tensor_tensor(out=xt[:, c0:c0 + 512], in0=xt[:, c0:c0 + 512], in1=tmp,
                                                                               op=ALU.add)), r=[("XT", h_)], w=["X"])
        P.barrier()
        if l == 0 and own:
            for j in range(2):
                dbg_dump("xmix0_%d" % j, X[:, j, :], [128, D], F32, keys=["X"])
            dbg_dump("xsmix0", XS[0:4, :], [4, D], F32, keys=["X"])
        mlp(l, tiles, rows, own)

    def mlp(l, tiles, rows, own):
        base = l * 6 * D
        rms_modulate(rows, norm_w[l, 1:2, :], base + 4 * D, base + 3 * D)
        for ti in tiles:
            to_feature_major(ti, HT[0:ntok(ti), ti, :])
        P.barrier()
        load_gate(l, 5, own)
        tblocks = [(0, 512), (512, 512)] + ([(TT, 4)] if own else [])
        for hc in range(DFF // 512):
            sU = load_w(mlp_up[l][:, hc * 512:(hc + 1) * 512], NCT, 512)
            for ht in range(4):
                for bi, (t0, nt) in enumerate(tblocks):
                    bank = 4 + (ht * 3 + bi) % 4
                    for k in range(NCT):
                        P.op("pe", (lambda e, k=k, sU=sU, bank=bank, ht=ht, t0=t0, nt=nt:
                                    e.matmul(PS[bank][:, 0:nt], lhsT=WB[:, sU, k, ht * 128:(ht + 1) * 128], rhs=HF[:, k, t0:t0 + nt],
                                             start=(k == 0), stop=(k == NCT - 1))), r=[("WB", sU), "HF"], w=[("PS", bank)])
                    P.op("act", (lambda e, bank=bank, ht=ht, t0=t0, nt=nt: e.activation(out=HID[:, ht, t0:t0 + nt], in_=PS[bank][:, 0:nt],
                                                                                        func=AF.Relu)), r=[("PS", bank)], w=[("HID", ht)])
                    P.op("pool", (lambda e, ht=ht, t0=t0, nt=nt: e.tensor_tensor(out=HID[:, ht, t0:t0 + nt], in0=HID[:, ht, t0:t0 + nt],
                                                                                in1=HID[:, ht, t0:t0 + nt], op=ALU.mult)),
                         r=[], w=[("HID", ht)])
            s = wslot[0]; wslot[0] ^= 1
            WD = WB[:, s].rearrange("p k n -> p (k n)").rearrange("p (a n) -> p a n", a=4)
            P.dma("pool", WD, mlp_down[l][hc * 512:(hc + 1) * 512, :].rearrange("(a p) n -> p a n", p=128), w=[("WB", s)])
            for ti in tiles:
                n = ntok(ti)
                for nb in range(4):
                    bank = (ti * 4 + nb) % 4
                    for kt in range(4):
                        P.op("pe", (lambda e, kt=kt, bank=bank, ti=ti, nb=nb, n=n, WD=WD:
                                    e.matmul(PS[bank][0:n, 0:512], lhsT=hid_cols(ti)[:, kt], rhs=WD[:, kt, nb * 512:(nb + 1) * 512],
                                             start=(kt == 0), stop=(kt == 3))), r=[("WB", s)] + [("HID", q) for q in range(4)],
                             w=[("PS", bank)])
                    resid_add(ti, nb * 512, 512, bank)
        P.barrier()
    KWT = sb("KWT", [128, 128], F32)
    SLW = sb("SLW", [128, 256], F32)
    LAMT = sb("LAMT", [128, 8], F32)
    ktsD = dscr("ktsD", [16, 128, 4], BF16)
    vsD = dscr("vsD", [4, D], BF16)
    JNK = USAMP

    def qk_norm_evac(ti, c0, bank, wt, out_dram):
        n = ntok(ti)
        h_ = xh[0]; xh[0] ^= 1
        tmp = XTMP[0:n, h_ * 512:(h_ + 1) * 512]
        sm = SM[0:n, 16 + 4 * h_:20 + 4 * h_]
        P.op("act", (lambda e, tmp=tmp, n=n: e.copy(out=tmp, in_=PS[bank][0:n, 0:512])), r=[("PS", bank)], w=[("XT", h_)])
        P.op("dve", (lambda e, sm=sm: e.memset(sm, 0.0)), r=[], w=[("SMQ", h_)])
        for q in range(4):
            P.op("act", (lambda e, q=q, n=n, tmp=tmp, h_=h_: e.activation(out=JNK[0:n, :], in_=tmp[:, q * 128:(q + 1) * 128], func=AF.Square,
                                                                       accum_out=SM[0:n, 16 + 4 * h_ + q:17 + 4 * h_ + q])),
                 r=[("XT", h_)], w=["JNK", ("SMQ", h_)])
        P.op("dve", (lambda e, sm=sm: e.tensor_scalar(out=sm, in0=sm, scalar1=1.0 / 128, scalar2=EPS, op0=ALU.mult, op1=ALU.add)),
             r=[], w=[("SMQ", h_)])
        P.op("act", (lambda e, sm=sm: e.activation(out=sm, in_=sm, func=AF.Sqrt)), r=[], w=[("SMQ", h_)])
        P.op("dve", (lambda e, sm=sm: e.reciprocal(out=sm, in_=sm)), r=[], w=[("SMQ", h_)])
        t3 = tmp.rearrange("p (q d) -> p q d", d=128)
        P.op("dve", (lambda e, t3=t3, sm=sm, n=n: e.tensor_tensor(out=t3, in0=t3, in1=sm.unsqueeze(2).broadcast_to([n, 4, 128]), op=ALU.mult)),
             r=[("SMQ", h_)], w=[("XT", h_)])
        P.op("dve", (lambda e, t3=t3, n=n: e.tensor_tensor(out=t3, in0=t3, in1=wt[0:n, :].unsqueeze(1).broadcast_to([n, 4, 128]), op=ALU.mult)),
             r=["KWT"], w=[("XT", h_)])
        if out_dram is not None:
            P.dma("sp", out_dram, tmp, r=[("XT", h_)])
        P.op("pool", (lambda e, tmp=tmp, n=n, ti=ti, c0=c0: e.tensor_copy(out=HT[0:n, ti, c0:c0 + 512], in_=tmp)), r=[("XT", h_)], w=["HT"])

    vst = [0]

    def kv_phase(own):
        tiles = list(range(8)) + ([8] if own else [])
        rows = [(list(range(8)), 0)] + ([([8], 1)] if own else [])
        base = 24 * D
        rms_modulate(rows, kv_norm_w[0:1, :], base + D, base)
        for ti in tiles:
            to_feature_major(ti, HT[0:ntok(ti), ti, :])
        bcast_row(KWT[:, :], k_norm_w[0:1, :], key="KWT")
        P.barrier()
        tokbase = TT if own else 0
        vDv = vD[tokbase:tokbase + TT, :].rearrange("(k j) d -> k j d", j=8)
        kov = k_o.rearrange("(k j) d -> k j d", j=8); vov = v_o.rearrange("(k j) d -> k j d", j=8)

        def evac(ti, c0, blk, bank):
            n = ntok(ti)
            if c0 < D:
                od = None
                if own:
                    od = kov[:, ti, c0:c0 + 512] if ti < 8 else k_s[:, c0:c0 + 512]
                qk_norm_evac(ti, c0, bank, KWT, od)
            else:
                c = c0 - D
                h_ = xh[0]; xh[0] ^= 1
                tmp = XTMP[0:n, h_ * 512:(h_ + 1) * 512]
                P.op("act", (lambda e, tmp=tmp, n=n: e.copy(out=tmp, in_=PS[bank][0:n, 0:512])), r=[("PS", bank)], w=[("XT", h_)])
                if own:
                    P.dma("sp", vov[:, ti, c:c + 512] if ti < 8 else v_s[:, c:c + 512], tmp, r=[("XT", h_)])
                vs_ = vst[0]; vst[0] = (vst[0] + 1) % 4
                stg = HID[0:n, vs_, 0:512]
                P.op("pool", (lambda e, stg=stg, tmp=tmp: e.tensor_copy(out=stg, in_=tmp)), r=[("XT", h_)], w=[("VST", vs_)])
                P.dma("sp", vDv[:, ti, c:c + 512] if ti < 8 else vsD[:, c:c + 512], stg, r=[("VST", vs_)], w=["vD"])

        linear_tok(kv_w, 2 * D, tiles, evac)
        P.barrier()
        for ti in tiles:
            to_feature_major(ti, HT[0:ntok(ti), ti, :])
        P.dma("sp", ktD[:, :, tokbase:tokbase + TT].rearrange("h d t -> d h t"), HF[:, :, 0:TT], r=["HF"], w=["ktD"])
        if own:
            P.dma("sp", ktsD.rearrange("h d t -> d h t"), HF[:, :, TT:TT + 4], r=["HF"], w=["ktD"])
        P.barrier()

    WBraw = WB[:].rearrange("p s k n -> p (s k n)")
    KTh = WBraw[:, 0:4096].rearrange("p (m t) -> p m t", m=2)
    VHa = WBraw[:, 4096:4096 + 16 * 258].rearrange("p (t v) -> p t v", v=258)
    ETL = WBraw[:, 8224:8224 + 4 * 128].rearrange("p (s q) -> p s q", s=4)
    KPf = WBraw[:, 0:4096].bitcast(F32)
    VPf = WBraw[:, 4096:8192].bitcast(F32)
    KPb = WBraw[:, 8192:10240]
    VPb = WBraw[:, 10240:12288]
    KTP = WBraw[:, 12288:14336].rearrange("p (c k) -> p c k", k=128)
    ESb = WBraw[:, 14336:14400]
    ESUM = WBraw[:, 14400:14528].bitcast(F32)
    ES4 = WBraw[:, 14528:14592]
    KTS = WBraw[:, 14592:14656].rearrange("p (c k) -> p c k", k=4)
    HIDf = HID[:].rearrange("p a b -> p (a b)")
    VSs = HIDf[:, 2048:4096]
    PTI = WBraw[:, 14656:14912].bitcast(I32)
    JNK2 = WBraw[:, 16000:16256]
    PTF = WBraw[:, 14912:15168].bitcast(F32)

    def build_pti():
        P.dma("sp", PTI, ptab[0:1, :].partition_broadcast(128)[:, 0, :], w=["PTI"])
        P.op("dve", lambda e: e.tensor_copy(out=PTF, in_=PTI), r=[], w=["PTI"])
        P.op("dve", lambda e: e.tensor_scalar(out=PTF, in0=PTF, scalar1=128.0, scalar2=FLG[:, 2:3], op0=ALU.mult, op1=ALU.add),
             r=["FLG"], w=["PTI"])
        P.op("dve", lambda e: e.tensor_copy(out=PTI, in_=PTF), r=[], w=["PTI"])

    def compute_lam(jb, l):
        lam_init = 0.8 - 0.6 * float(np.exp(-0.3 * l))
        for idx, (a_, b_) in enumerate(((lq1, lk1), (lq2, lk2))):
            bcast_row(XTMP[:, 0:128], a_[jb:jb + 1, :], key="XTMP")
            bcast_row(XTMP[:, 128:256], b_[jb:jb + 1, :], key="XTMP")
            P.op("dve", lambda e: e.tensor_tensor(out=XTMP[:, 256:384], in0=XTMP[:, 0:128], in1=XTMP[:, 128:256], op=ALU.mult),
                 r=["XTMP"], w=["XTMP"])
            P.op("dve", (lambda e, idx=idx: e.tensor_reduce(out=LAMT[:, 2 + idx:3 + idx], in_=XTMP[:, 256:384], axis=AX.X, op=ALU.add)),
                 r=["XTMP"], w=["LAMT"])
            P.op("act", (lambda e, idx=idx: e.activation(out=LAMT[:, 2 + idx:3 + idx], in_=LAMT[:, 2 + idx:3 + idx], func=AF.Exp)),
                 r=[], w=["LAMT"])
        P.op("dve", lambda e: e.tensor_tensor(out=LAMT[:, 0:1], in0=LAMT[:, 2:3], in1=LAMT[:, 3:4], op=ALU.subtract), r=[], w=["LAMT"])
        P.op("dve", lambda e: e.tensor_scalar(out=LAMT[:, 0:1], in0=LAMT[:, 0:1], scalar1=lam_init, scalar2=None, op0=ALU.add), r=[], w=["LAMT"])
        P.op("dve", lambda e: e.tensor_scalar(out=LAMT[:, 1:2], in0=LAMT[:, 0:1], scalar1=-1.0, scalar2=None, op0=ALU.mult), r=[], w=["LAMT"])
        return lam_init

    def subln_store(src, n, dst, lam_init, smcol):
        sm = SM[0:n, smcol:smcol + 1]
        P.op("dve", (lambda e, sm=sm: e.memset(sm, 0.0)), r=[], w=["SMS"])
        P.op("act", (lambda e, sm=sm, n=n, src=src: e.activation(out=JNK2[0:n, :], in_=src,
                                                               func=AF.Square, accum_out=sm)), r=["OTMP"], w=["SMS", "JNK2"])
        P.op("dve", (lambda e, sm=sm: e.tensor_scalar(out=sm, in0=sm, scalar1=1.0 / 256, scalar2=EPS, op0=ALU.mult, op1=ALU.add)),
             r=[], w=["SMS"])
        P.op("act", (lambda e, sm=sm: e.activation(out=sm, in_=sm, func=AF.Sqrt)), r=[], w=["SMS"])
        P.op("dve", (lambda e, sm=sm: e.reciprocal(out=sm, in_=sm)), r=[], w=["SMS"])
        P.op("dve", (lambda e, sm=sm, src=src: e.tensor_scalar(out=src, in0=src, scalar1=sm, scalar2=1.0 - lam_init, op0=ALU.mult, op1=ALU.mult)),
             r=["SMS"], w=["OTMP"])
        P.op("dve", (lambda e, src=src, n=n, dst=dst: e.tensor_tensor(out=dst, in0=src, in1=SLW[0:n, :], op=ALU.mult)),
             r=["OTMP", "SLW"], w=["HT"])

    def sample_attention(lam_init):
        build_pti()
        P.dma("sp", KTS, ktsD.rearrange("h d t -> d h t"), w=["KTS"])
        P.dma("sp", VSs[0:4, :], vsD[:, :], w=["VSs"])
        P.op("dve", lambda e: e.memset(ESUM, 0.0), w=["ESUM"])
        QTs = HF[:, :, TT:TT + 4]
        for pg in range(NPAGES + 1):
            last = pg == NPAGES
            if not last:
                P.dmafn("pool", (lambda e, pg=pg: e.indirect_dma_start(
                    out=KPf, out_offset=None, in_=cache_k[:, :],
                    in_offset=bass.IndirectOffsetOnAxis(ap=PTI[:, pg:pg + 1], axis=0))), r=["PTI"], w=["KPf"])
                P.dmafn("pool", (lambda e, pg=pg: e.indirect_dma_start(
                    out=VPf, out_offset=None, in_=cache_v[:, :],
                    in_offset=bass.IndirectOffsetOnAxis(ap=PTI[:, pg:pg + 1], axis=0))), r=["PTI"], w=["VPf"])
                P.op("act", lambda e: e.copy(out=KPb, in_=KPf), r=["KPf"], w=["KPb"])
                P.op("dve", lambda e: e.tensor_copy(out=VPb, in_=VPf), r=["VPf"], w=["VPb"])
                for half in range(2):
                    bank = 4 + half
                    for c in range(8):
                        hm = half * 8 + c
                        P.op("pe", (lambda e, hm=hm, c=c, bank=bank: e.transpose(out=psbf(bank)[:, c * 128:(c + 1) * 128],
                                                                                in_=KPb[:, hm * 128:(hm + 1) * 128], identity=IDB[:])),
                             r=["KPb", "IDB"], w=[("PS", bank)])
                    P.op("dve" if half else "act",
                         (lambda e, half=half, bank=bank: (e.tensor_copy if half else e.copy)(
                             out=KTP[:, half * 8:(half + 1) * 8, :], in_=psbf(bank)[:, 0:1024].rearrange("p (c k) -> p c k", k=128))),
                         r=[("PS", bank)], w=["KTP"])
                for hm in range(16):
                    P.op("pe", (lambda e, hm=hm: e.matmul(PS[6][:, hm * 4:(hm + 1) * 4], lhsT=KTP[:, hm, :], rhs=QTs[:, hm, :],
                                                          start=True, stop=True)), r=["KTP", "HF"], w=[("PS", 6)])
                P.op("act", lambda e: e.activation(out=ESb, in_=PS[6][:, 0:64], func=AF.Exp, scale=SCALE), r=[("PS", 6)], w=["ESb"])
                P.op("pool", lambda e: e.tensor_tensor(out=ESUM, in0=ESUM, in1=ESb, op=ALU.add), r=["ESb"], w=["ESUM"])
                for h in range(NH):
                    P.op("pe", (lambda e, h=h, pg=pg: e.matmul(PS[h // 2][0:8, (h % 2) * 256:(h % 2 + 1) * 256], lhsT=ESb[:, h * 8:(h + 1) * 8],
                                                               rhs=VPb[:, h * 256:(h + 1) * 256], start=(pg == 0), stop=False)),
                         r=["ESb", "VPb"], w=[("PS", h // 2)])
            else:
                for hm in range(16):
                    P.op("pe", (lambda e, hm=hm: e.matmul(PS[6][0:4, hm * 4:(hm + 1) * 4], lhsT=KTS[:, hm, :], rhs=QTs[:, hm, :],
                                                          start=True, stop=True)), r=["KTS", "HF"], w=[("PS", 6)])
                P.op("act", lambda e: e.activation(out=ES4[0:4, :], in_=PS[6][0:4, 0:64], func=AF.Exp, scale=SCALE), r=[("PS", 6)], w=["ES4"])
                P.op("dve", lambda e: e.tensor_tensor(out=ES4[0:4, :].rearrange("p (c q) -> p c q", q=4),
                                                      in0=ES4[0:4, :].rearrange("p (c q) -> p c q", q=4),
                                                      in1=CAUS[0:4, 0:4].unsqueeze(1).broadcast_to([4, 16, 4]), op=ALU.mult),
                     r=["CAUS"], w=["ES4"])
                P.op("pool", lambda e: e.tensor_tensor(out=ESUM[0:4, :], in0=ESUM[0:4, :], in1=ES4[0:4, :], op=ALU.add), r=["ES4"], w=["ESUM"])
                for h in range(NH):
                    P.op("pe", (lambda e, h=h: e.matmul(PS[h // 2][0:8, (h % 2) * 256:(h % 2 + 1) * 256], lhsT=ES4[0:4, h * 8:(h + 1) * 8],
                                                        rhs=VSs[0:4, h * 256:(h + 1) * 256], start=False, stop=True)),
                         r=["ES4", "VSs"], w=[("PS", h // 2)])
        P.op("dve", lambda e: e.tensor_copy(out=ESb, in_=ESUM), r=["ESUM"], w=["ESb"])
        for h in range(NH):
            P.op("pe", (lambda e, h=h: e.matmul(PS[7][0:8, h:h + 1], lhsT=ESUM[:, h * 8:(h + 1) * 8], rhs=ONESF[:, 0:1],
                                                start=True, stop=True)), r=["ESUM", "ONESF"], w=[("PS", 7)])
        DEN = SM[0:8, 32:40]
        P.op("dve", lambda e: e.reciprocal(out=DEN, in_=PS[7][0:8, 0:8]), r=[("PS", 7)], w=["DEN"])
        OT = XTMP[0:8, :]
        for hq in range(2):
            for hh in range(4):
                h = hq * 4 + hh
                P.op("dve", (lambda e, h=h, hh=hh: e.tensor_scalar(out=OT[:, hh * 256:(hh + 1) * 256], in0=PS[h // 2][0:8, (h % 2) * 256:(h % 2 + 1) * 256],
                                                                   scalar1=DEN[:, h:h + 1], scalar2=None, op0=ALU.mult)),
                     r=[("PS", h // 2), "DEN"], w=["XTMP"])
            P.dma("sp", OM1[0:4, :], OT[4:8, :], r=["XTMP"], w=["OM1"])
            for hh in range(4):
                h = hq * 4 + hh
                src = OT[0:4, hh * 256:(hh + 1) * 256]
                P.op("dve", (lambda e, src=src, hh=hh: e.scalar_tensor_tensor(out=src, in0=OM1[0:4, hh * 256:(hh + 1) * 256], scalar=LAMT[0:4, 1:2],
                                                                             in1=src, op0=ALU.mult, op1=ALU.add)),
                     r=["OM1", "LAMT", "XTMP"], w=["XTMP", "OTMP"])
                subln_store(src, 4, HT[0:4, 8, h * 256:(h + 1) * 256], lam_init, 40 + h)
            P.barrier()
        P.barrier()

    ONESF = sb("ONESF", [128, 2], F32)
    P.op("pool", lambda e: e.memset(ONESF[:, :], 1.0), w=["ONESF"])
    OM1 = HIDf[0:4, 0:2048].bitcast(F32)

    def prompt_attention(lam_init):
        P.op("pool", lambda e: e.memset(VHa[:, :, 256:258], 1.0), w=["VHa"])
        vDn = vD.rearrange("(t p) d -> p t d", p=128)
        for h in range(NH):
            P.dma("sp", KTh, ktD[2 * h:2 * h + 2].rearrange("m d t -> d m t"), r=["ktD"], w=["KTh"])
            P.dma("sp", VHa[:, :, 0:256], vDn[:, :, h * 256:(h + 1) * 256], r=["vD"], w=["VHa"])
            for qi in range(8):
                nkt = 8 + qi + 1
                ob = 2 + 2 * (qi % 2)
                for m in range(2):
                    for kt in range(nkt):
                        sbk = kt % 2
                        es = (m * 17 + kt) % 4
                        P.op("pe", (lambda e, m=m, kt=kt, sbk=sbk, h=h, qi=qi: e.matmul(
                            PS[sbk][:, 0:128], lhsT=KTh[:, m, kt * 128:(kt + 1) * 128], rhs=HF[:, 2 * h + m, qi * 128:(qi + 1) * 128],
                            start=True, stop=True)), r=["KTh", "HF"], w=[("PS", sbk)])
                        if kt < 8:
                            P.op("act", (lambda e, sbk=sbk, es=es: e.activation(out=ETL[:, es, :], in_=PS[sbk][:, 0:128], func=AF.Exp,
                                                                              scale=SCALE, bias=FLG[:, 1:2])), r=[("PS", sbk), "FLG"], w=[("ET", es)])
                        else:
                            P.op("act", (lambda e, sbk=sbk, es=es: e.activation(out=ETL[:, es, :], in_=PS[sbk][:, 0:128], func=AF.Exp,
                                                                              scale=SCALE)), r=[("PS", sbk)], w=[("ET", es)])
                        if kt == nkt - 1:
                            P.op("pool", (lambda e, es=es: e.tensor_tensor(out=ETL[:, es, :], in0=ETL[:, es, :], in1=CAUS[:, :], op=ALU.mult)),
                                 r=["CAUS"], w=[("ET", es)])
                        P.op("pe", (lambda e, m=m, kt=kt, es=es, ob=ob, nkt=nkt: e.matmul(
                            PS[ob + m][:, 0:257], lhsT=ETL[:, es, :], rhs=VHa[:, kt, 0:257], start=(kt == 0), stop=(kt == nkt - 1))),
                            r=[("ET", es), "VHa"], w=[("PS", ob + m)])
                h_ = xh[0]; xh[0] ^= 1
                src = XTMP[:, h_ * 512:h_ * 512 + 256]
                rr = SM[:, 48 + 2 * h_:50 + 2 * h_]
                P.op("dve", (lambda e, rr=rr, ob=ob: e.reciprocal(out=rr[:, 0:1], in_=PS[ob][:, 256:257])), r=[("PS", ob)], w=[("RR", h_)])
                P.op("dve", (lambda e, rr=rr, ob=ob: e.reciprocal(out=rr[:, 1:2], in_=PS[ob + 1][:, 256:257])), r=[("PS", ob + 1)], w=[("RR", h_)])
                P.op("dve", (lambda e, rr=rr: e.tensor_tensor(out=rr[:, 1:2], in0=rr[:, 1:2], in1=LAMT[:, 1:2], op=ALU.mult)), r=["LAMT"], w=[("RR", h_)])
                P.op("dve", (lambda e, rr=rr, ob=ob, src=src: e.tensor_scalar(out=src, in0=PS[ob][:, 0:256], scalar1=rr[:, 0:1], scalar2=None,
                                                                              op0=ALU.mult)), r=[("PS", ob), ("RR", h_)], w=["OTMP", ("XT", h_)])
                P.op("dve", (lambda e, rr=rr, ob=ob, src=src: e.scalar_tensor_tensor(out=src, in0=PS[ob + 1][:, 0:256], scalar=rr[:, 1:2], in1=src,
                                                                                     op0=ALU.mult, op1=ALU.add)),
                     r=[("PS", ob + 1), ("RR", h_)], w=["OTMP", ("XT", h_)])
                subln_store(src, 128, HT[:, qi, h * 256:(h + 1) * 256], lam_init, 56 + h_)
        P.barrier()

    def b_layer(l):
        jb = l - NA
        tiles = list(range(9))
        rows = [(list(range(8)), 0), ([8], 1)]
        base = l * 6 * D
        rms_modulate(rows, norm_w[l, 0:1, :], base + D, base)
        for ti in tiles:
            to_feature_major(ti, HT[0:ntok(ti), ti, :])
        bcast_row(KWT[:, :], q_norm_w[jb:jb + 1, :], key="KWT")
        bcast_row(SLW[:, :], subln_w[jb:jb + 1, :], key="SLW")
        lam_init = compute_lam(jb, l)
        P.barrier()
        linear_tok(attn_wq[jb], D, tiles, lambda ti, c0, blk, bank: qk_norm_evac(ti, c0, bank, KWT, None))
        P.barrier()
        for ti in tiles:
            to_feature_major(ti, HT[0:ntok(ti), ti, :])
        P.barrier()
        sample_attention(lam_init)
        prompt_attention(lam_init)
        for qi in range(8):
            to_feature_major(qi, HT[:, qi, :], dstF=HF[:, :, qi * 128:(qi + 1) * 128])
        to_feature_major(8, HT[0:4, 8, :])
        P.barrier()
        load_gate(l, 2, True)
        linear_tok(attn_wo[jb], D, tiles, lambda ti, c0, blk, bank: resid_add(ti, c0, blk, bank))
        P.barrier()
        mlp(l, tiles, rows, True)

    load_x(xp, False)
    a_layer(0, False)
    a_layer(1, False)
    kv_phase(False)
    load_x(xo, True)
    a_layer(0, True)
    a_layer(1, True)
    kv_phase(True)
    b_layer(2)
    b_layer(3)
    ov = y_o.rearrange("(k j) d -> k j d", j=8)
    for j in range(8):
        P.dma("sp", ov[:, j, :], X[:, j, :], r=["X"])
    P.dma("sp", y_s[:, :], XS[0:4, :], r=["X"])
    P.emit(nc, stack)
    return nc, stack


_CACHE = {}


def kernel(x_prompt, x_sample, state_ssm_re, state_ssm_im, cache_k, cache_v, page_table, c_prompt, c_sample,
           ada_w, ada_b, norm_w, mlp_up, mlp_down, ssm_lambda_re, ssm_lambda_im, ssm_log_step,
           ssm_b_re, ssm_b_im, ssm_c_re, ssm_c_im, ssm_d, glu_w, kv_ada_w, kv_ada_b, kv_norm_w, kv_w,
           k_norm_w, attn_wq, q_norm_w, lambda_q1, lambda_k1, lambda_q2, lambda_k2, subln_w, attn_wo):
    f32 = np.float32
    A = lambda a: np.ascontiguousarray(np.asarray(a))
    if "nc" not in _CACHE:
        _CACHE["nc"] = build_program()
    nc, _stack = _CACHE["nc"]
    x_prompt = A(x_prompt); x_sample = A(x_sample)
    ck = A(cache_k).reshape(1280 * 128, D); cv = A(cache_v).reshape(1280 * 128, D)
    ident = np.eye(128, dtype=f32)
    idx = np.arange(128)
    tri = (idx[:, None] // 16 <= idx[None, :] // 16).astype(f32)
    caus = (idx[:, None] <= idx[None, :]).astype(f32)
    shared = {
        "cache_k": ck, "cache_v": cv,
        "ada_w": A(ada_w), "ada_b": A(ada_b), "norm_w": A(norm_w), "mlp_up": A(mlp_up), "mlp_down": A(mlp_down),
        "ssm_lambda_re": A(ssm_lambda_re), "ssm_lambda_im": A(ssm_lambda_im), "ssm_log_step": A(ssm_log_step),
        "ssm_b_re": A(ssm_b_re), "ssm_b_im": A(ssm_b_im), "ssm_c_re": A(ssm_c_re), "ssm_c_im": A(ssm_c_im),
        "ssm_d": A(ssm_d), "glu_w": A(glu_w), "kv_ada_w": A(kv_ada_w), "kv_ada_b": A(kv_ada_b).reshape(1, -1),
        "kv_norm_w": A(kv_norm_w).reshape(1, -1), "kv_w": A(kv_w), "k_norm_w": A(k_norm_w).reshape(1, -1),
        "attn_wq": A(attn_wq), "q_norm_w": A(q_norm_w), "lambda_q1": A(lambda_q1), "lambda_k1": A(lambda_k1),
        "lambda_q2": A(lambda_q2), "lambda_k2": A(lambda_k2), "subln_w": A(subln_w), "attn_wo": A(attn_wo),
        "cident": ident, "ctri": tri, "ccaus": caus,
    }
    zeros_x = np.zeros((TT, D), f32)
    in_maps = []
    for i in range(8):
        b, half = i // 2, i % 2
        fl = np.zeros((128, 3), f32)
        fl[:, 0] = float(half); fl[:, 1] = (float(half) - 1.0) * 30000.0; fl[:, 2] = np.arange(128)
        m = dict(shared)
        m.update({
            "xo": A(x_prompt[b, half * TT:(half + 1) * TT]),
            "xp": A(x_prompt[b, 0:TT]) if half == 1 else zeros_x,
            "xs": A(x_sample[i]),
            "cvec": A(np.stack([np.asarray(c_prompt)[b], np.asarray(c_sample)[i]]).astype(f32)),
            "s0re": A(np.asarray(state_ssm_re)[:, i]), "s0im": A(np.asarray(state_ssm_im)[:, i]),
            "ptab": A(np.asarray(page_table)[i:i + 1].astype(np.int32)),
            "flag": fl,
        })
        in_maps.append(m)
    res = run_bass_kernel_spmd(nc, in_maps, core_ids=list(range(8)))
    R = res.results
    _CACHE["res"] = R
    B, T = x_prompt.shape[0], x_prompt.shape[1]
    y_prompt = np.zeros((B, T, D), f32); k_prompt = np.zeros((B, T, NH, 256), f32); v_prompt = np.zeros((B, T, NH, 256), f32)
    spr = np.zeros((NA, B, G, NST), f32); spi = np.zeros((NA, B, G, NST), f32)
    y_sample = np.zeros((8, 4, D), f32); k_sample = np.zeros((8, 4, NH, 256), f32); v_sample = np.zeros((8, 4, NH, 256), f32)
    ssr = np.zeros((NA, 8, G, NST), f32); ssi = np.zeros((NA, 8, G, NST), f32)
    for i in range(8):
        b, half = i // 2, i % 2
        sl = slice(half * TT, (half + 1) * TT)
        y_prompt[b, sl] = R[i]["y_o"]
        k_prompt[b, sl] = R[i]["k_o"].reshape(TT, NH, 256)
        v_prompt[b, sl] = R[i]["v_o"].reshape(TT, NH, 256)
        if half == 1:
            spr[:, b] = R[i]["sp_re"]; spi[:, b] = R[i]["sp_im"]
        y_sample[i] = R[i]["y_s"]
        k_sample[i] = R[i]["k_s"].reshape(4, NH, 256); v_sample[i] = R[i]["v_s"].reshape(4, NH, 256)
        ssr[:, i] = R[i]["ss_re"]; ssi[:, i] = R[i]["ss_im"]
    return (y_prompt, y_sample, spr, spi, k_prompt, v_prompt, ssr, ssi, k_sample, v_sample)
```
